# Optimizing a Trainium2 kernel written in Bass

```python
import jax, jax.numpy as jnp
from jax import lax
import numpy as np

D_MODEL = 1024
BATCH = 8
SEQ = 4096
DEPTH = 1

CHUNK = 64
N_HEADS_SB = 8
HEAD_DIM_SB = 64
D_SB = N_HEADS_SB * HEAD_DIM_SB
N_GROUPS_SGU = 8
GROUP_DIM_SGU = 64
D_SGU = N_GROUPS_SGU * GROUP_DIM_SGU
SGU_CHUNK = 128
Q_BLOCK = 128
EPS = 1e-6
IN_WIDTHS = (D_SB, D_SB, D_SB, D_SB, D_SGU, D_SGU, D_SGU, D_MODEL, D_MODEL)
D_IN = 4 * D_SB + 3 * D_SGU + 2 * D_MODEL

kernel_name = 'stickbreak_sgu_gated_hybrid'


def rmsnorm(x, g):
    xf = x.astype(jnp.float32)
    y = xf * lax.rsqrt(jnp.mean(xf * xf, axis=-1, keepdims=True) + EPS)
    return (y * g.astype(jnp.float32)).astype(x.dtype)


def split_points():
    pts, acc = [], 0
    for w in IN_WIDTHS[:-1]:
        acc += w
        pts.append(acc)
    return pts


def stick_breaking_attention(q, k, v):
    b, s, h, dh = q.shape
    scale = dh ** -0.5
    qf = q.astype(jnp.float32).transpose(0, 2, 1, 3) * scale
    kf = k.astype(jnp.float32).transpose(0, 2, 1, 3)
    vf = v.astype(jnp.float32).transpose(0, 2, 1, 3)
    outs = []
    for start in range(0, s, Q_BLOCK):
        end = start + Q_BLOCK
        qb = qf[:, :, start:end]
        kb = kf[:, :, :end]
        vb = vf[:, :, :end]
        z = jnp.einsum('bhqd,bhkd->bhqk', qb, kb)
        t_idx = start + jnp.arange(Q_BLOCK)[:, None]
        s_idx = jnp.arange(end)[None, :]
        before = s_idx < t_idx
        log_keep = jnp.where(before, jax.nn.log_sigmoid(-z), 0.0)
        log_stick = lax.cumsum(log_keep, axis=3, reverse=True) - log_keep
        log_w = jax.nn.log_sigmoid(z) + log_stick
        w = jnp.where(before, jnp.exp(log_w), 0.0)
        outs.append(jnp.einsum('bhqk,bhkd->bhqd', w, vb))
    o = jnp.concatenate(outs, axis=2)
    return o.transpose(0, 2, 1, 3).astype(q.dtype)


def spatial_gating(u, v, ln_g, ln_b, w_s, b_s):
    b, s, g, c = v.shape
    vf = v.astype(jnp.float32)
    mu = jnp.mean(vf, axis=-1, keepdims=True)
    var = jnp.mean(jnp.square(vf - mu), axis=-1, keepdims=True)
    vn = (vf - mu) * lax.rsqrt(var + EPS) * ln_g.astype(jnp.float32) + ln_b.astype(jnp.float32)
    vn = vn.reshape(b, s // SGU_CHUNK, SGU_CHUNK, g, c)
    pos = jnp.arange(SGU_CHUNK)
    mask = (pos[None, :] // CHUNK) <= (pos[:, None] // CHUNK)
    w = jnp.where(mask[None], w_s.astype(jnp.float32), 0.0)
    mixed = jnp.einsum('gts,bnsgc->bntgc', w, vn) + b_s.astype(jnp.float32).T[None, None, :, :, None]
    return u * mixed.reshape(b, s, g, c).astype(u.dtype)


def setup_inputs(seed: int = 0) -> dict:
    key = jax.random.key(seed)
    ks = jax.random.split(key, 13)
    f32 = jnp.float32
    x = jax.random.normal(ks[0], (BATCH, SEQ, D_MODEL), f32)
    norm_g = 1.0 + 0.1 * jax.random.normal(ks[1], (DEPTH, D_MODEL), f32)
    w_in = jax.random.normal(ks[2], (DEPTH, D_MODEL, D_IN), f32) * D_MODEL ** -0.5
    sgu_ln_g = 1.0 + 0.1 * jax.random.normal(ks[3], (DEPTH, N_GROUPS_SGU, GROUP_DIM_SGU), f32)
    sgu_ln_b = 0.1 * jax.random.normal(ks[4], (DEPTH, N_GROUPS_SGU, GROUP_DIM_SGU), f32)
    w_spatial = jax.random.normal(ks[5], (DEPTH, N_GROUPS_SGU, SGU_CHUNK, SGU_CHUNK), f32) * SGU_CHUNK ** -0.5
    b_spatial = 1.0 + 0.1 * jax.random.normal(ks[6], (DEPTH, N_GROUPS_SGU, SGU_CHUNK), f32)
    w_up_a = jax.random.normal(ks[7], (DEPTH, D_SB, D_MODEL), f32) * D_SB ** -0.5
    w_up_b = jax.random.normal(ks[8], (DEPTH, D_SGU, D_MODEL), f32) * D_SGU ** -0.5
    w_out = jax.random.normal(ks[9], (DEPTH, D_MODEL, D_MODEL), f32) * D_MODEL ** -0.5
    final_norm_g = 1.0 + 0.1 * jax.random.normal(ks[10], (D_MODEL,), f32)
    return {'x': x, 'norm_g': norm_g, 'w_in': w_in, 'sgu_ln_g': sgu_ln_g, 'sgu_ln_b': sgu_ln_b,
            'w_spatial': w_spatial, 'b_spatial': b_spatial, 'w_up_a': w_up_a, 'w_up_b': w_up_b,
            'w_out': w_out, 'final_norm_g': final_norm_g}


def reference(x, norm_g, w_in, sgu_ln_g, sgu_ln_b, w_spatial, b_spatial, w_up_a, w_up_b, w_out, final_norm_g):
    b, s, _ = x.shape
    pts = split_points()
    for l in range(DEPTH):
        h = rmsnorm(x, norm_g[l])
        proj = jnp.einsum('bsd,de->bse', h, w_in[l])
        q, k, v, z_a, u_b, v_b, z_b, g_a, g_b = jnp.split(proj, pts, axis=-1)
        y_a = stick_breaking_attention(
            q.reshape(b, s, N_HEADS_SB, HEAD_DIM_SB),
            k.reshape(b, s, N_HEADS_SB, HEAD_DIM_SB),
            v.reshape(b, s, N_HEADS_SB, HEAD_DIM_SB)).reshape(b, s, D_SB) * jax.nn.silu(z_a)
        y_b = spatial_gating(
            jax.nn.gelu(u_b).reshape(b, s, N_GROUPS_SGU, GROUP_DIM_SGU),
            jax.nn.gelu(v_b).reshape(b, s, N_GROUPS_SGU, GROUP_DIM_SGU),
            sgu_ln_g[l], sgu_ln_b[l], w_spatial[l], b_spatial[l]).reshape(b, s, D_SGU) * jax.nn.silu(z_b)
        p_a = jnp.einsum('bse,ed->bsd', y_a, w_up_a[l])
        p_b = jnp.einsum('bse,ed->bsd', y_b, w_up_b[l])
        merged = jax.nn.sigmoid(g_a) * p_a + jax.nn.sigmoid(g_b) * p_b
        x = x + jnp.einsum('bsd,de->bse', merged, w_out[l])
    return rmsnorm(x, final_norm_g)
```

```python
import numpy as np
import concourse.bass as bass
import concourse.mybir as mybir
from concourse.bass_utils import run_bass_kernel_spmd

F32 = mybir.dt.float32
BF16 = mybir.dt.bfloat16
AF = mybir.ActivationFunctionType
ALU = mybir.AluOpType
AX = mybir.AxisListType

D = 1024
DIN = 5632
EPS = 1e-6
NEG = -30000.0
C_G1 = 0.7978845608028654
C_G2 = 0.044715
N_DMA_SEMS = 16
INF = 1 << 60
KB = 1024


class Op:
    __slots__ = ("eng", "fn", "deps", "dma", "signal", "val", "sem", "prev_val", "waits", "idx")


class Sched:
    ENGS = ("pe", "act", "dve", "pool", "sp")

    def __init__(self, nc):
        self.nc = nc
        self.q = {e: [] for e in self.ENGS}
        self.all = []
        self.acc = {}
        self.esem = {}
        self.dsems = []
        self._cap = None

    @staticmethod
    def _norm(k):
        if isinstance(k, tuple):
            return k[0], k[1], k[2]
        return k, 0, INF

    def capture(self):
        self._cap = []

    def end_capture(self):
        c, self._cap = self._cap, None
        return c

    def interleave(self, chains):
        idx = [0] * len(chains)
        left = sum(len(c) for c in chains)
        while left:
            for i, c in enumerate(chains):
                if idx[i] < len(c):
                    self.add(*c[idx[i]])
                    idx[i] += 1
                    left -= 1

    def add(self, eng, fn, r=(), w=(), extra=(), dma=False):
        if self._cap is not None:
            self._cap.append((eng, fn, tuple(r), tuple(w), tuple(extra), dma))
            return None
        op = Op()
        op.eng, op.fn, op.dma = eng, fn, dma
        op.signal, op.val, op.sem, op.prev_val, op.waits = False, 0, None, 0, []
        op.idx = len(self.all)
        deps = set(e for e in extra if e is not None)
        rn = [self._norm(k) for k in r]
        wn = [self._norm(k) for k in w]
        for name, lo, hi in rn:
            for rec in self.acc.get(name, ()):
                if rec[3] and rec[0] < hi and lo < rec[1]:
                    deps.add(rec[2])
        for name, lo, hi in wn:
            for rec in self.acc.get(name, ()):
                if rec[0] < hi and lo < rec[1]:
                    deps.add(rec[2])
        for name, lo, hi in rn:
            self.acc.setdefault(name, []).append([lo, hi, op, False])
        for name, lo, hi in wn:
            lst = self.acc.setdefault(name, [])
            lst[:] = [rec for rec in lst if not (lo <= rec[0] and rec[1] <= hi)]
            lst.append([lo, hi, op, True])
        deps.discard(op)
        op.deps = deps
        self.q[eng].append(op)
        self.all.append(op)
        return op

    def finalize(self):
        nc = self.nc
        for e in self.ENGS:
            self.esem[e] = nc.alloc_semaphore("sem_" + e)
        self.dsems = [nc.alloc_semaphore("dsem%d" % i) for i in range(N_DMA_SEMS)]
        for op in self.all:
            for d in op.deps:
                if d.dma:
                    continue
                if d.eng == op.eng and d.eng == "pe":
                    continue
                d.signal = True
        uses = [0] * N_DMA_SEMS
        ndma = 0
        for e in self.ENGS:
            cnt = 0
            for op in self.q[e]:
                if op.dma:
                    i = ndma % N_DMA_SEMS
                    ndma += 1
                    op.sem = self.dsems[i]
                    op.prev_val = 16 * uses[i]
                    uses[i] += 1
                    op.val = 16 * uses[i]
                elif op.signal:
                    cnt += 1
                    op.val = cnt
        for e in self.ENGS:
            waited = {}
            for op in self.q[e]:
                need = {}
                for d in op.deps:
                    if d.dma:
                        s, v = d.sem, d.val
                    else:
                        if d.eng == e and e == "pe":
                            continue
                        s, v = self.esem[d.eng], d.val
                    if need.get(s.num, (None, 0))[1] < v:
                        need[s.num] = (s, v)
                if op.dma and op.prev_val > 0:
                    if need.get(op.sem.num, (None, 0))[1] < op.prev_val:
                        need[op.sem.num] = (op.sem, op.prev_val)
                op.waits = []
                for num, (s, v) in need.items():
                    if waited.get(num, 0) < v:
                        op.waits.append((s, v))
                        waited[num] = v

    def emit(self, e, eng):
        for op in self.q[e]:
            for (s, v) in op.waits:
                eng.wait_ge(s, v)
            ins = op.fn(eng)
            if op.dma:
                ins.then_inc(op.sem, 16)
            elif op.signal:
                ins.then_inc(self.esem[e], 1)


def PK(bank, ph=None, c0=0, c1=512):
    name = "ps%d" % bank
    halves = (0, 1) if ph is None else (ph,)
    return [(name, h * 1024 + c0, h * 1024 + c1) for h in halves]


def build_nc(S):
    NB = S // 128
    NT = S // 512
    NT3 = S // 256
    nc = bass.Bass("TRN2", target_bir_lowering=False)

    def din(name, shape):
        return nc.dram_tensor(name, list(shape), F32, kind="ExternalInput").ap()

    x_d = din("x", (S, D))
    win_d = din("w_in", (D, DIN))
    wua_d = din("w_up_a", (512, D))
    wub_d = din("w_up_b", (512, D))
    wout_d = din("w_out", (D, D))
    g1_d = din("g1", (128, 8))
    fg_d = din("fg", (128, D))
    lg_d = din("lg", (128, 8))
    bsT_d = din("bsT", (128, 512))
    wsT_d = din("wsT", (128, 1024))
    msk_d = din("msk", (128, 1024))
    cst_d = din("cst", (128, 1280))
    out_d = nc.dram_tensor("out", [S, D], F32, kind="ExternalOutput").ap()

    sc = Sched(nc)

    ARENA_E = 49152
    UBYTES = 72 * KB
    arena_h = nc.alloc_sbuf_tensor("arena", [128, ARENA_E], BF16)
    ya_h = nc.alloc_sbuf_tensor("yaT", [128, 4 * S], BF16)
    cstb_h = nc.alloc_sbuf_tensor("cstb", [128, 1280], BF16)
    g1_h = nc.alloc_sbuf_tensor("g1s", [128, 8], F32)
    fg_h = nc.alloc_sbuf_tensor("fgs", [128, D], F32)
    st_h = nc.alloc_sbuf_tensor("st", [128, 256], F32)
    nh_h = nc.alloc_sbuf_tensor("nh", [128, 8], F32)
    U_h = nc.alloc_sbuf_tensor("U", [128, UBYTES // 4], F32)
    LP = [nc.alloc_psum_tensor("lp%d" % j, [128, 1024], F32)[:] for j in range(3)]
    psb = []
    for j in range(3):
        psb += [LP[j][:, 0:512], LP[j][:, 512:1024]]
    psb += [nc.alloc_psum_tensor("ps%d" % i, [128, 512], F32)[:] for i in range(6, 8)]

    arena = arena_h[:]
    U = U_h[:]
    cstb = cstb_h[:]
    st = st_h[:]
    fg = fg_h[:]
    ident = cstb[:, 0:128]
    nUi = cstb[:, 128:256]
    nOnes = cstb[:, 256:384]
    NEGW = cstb[:, 384:1280]

    SL = 4 * S
    QO, KO, VO = 0, 16384, 32768
    qT = arena[:, QO:QO + SL].rearrange("p (c t) -> p c t", c=4)
    kT = arena[:, KO:KO + SL].rearrange("p (c t) -> p c t", c=4)
    vv = arena[:, VO:VO + SL].rearrange("p (b f) -> p b f", f=512)
    yaT = ya_h[:].rearrange("p (c t) -> p c t", c=4)
    W3OFF = [0, 3584, 7168, 16384, 19968, 23552, 28672, 32256]
    W3c = [arena[:, o:o + 3584] for o in W3OFF]
    W3KEYS = [("arena", o, o + 3584) for o in W3OFF]
    WOUT_O, WUA_O, WUB_O = 35840, 44032, 12288
    WUA = arena[:, WUA_O:WUA_O + 4096].rearrange("p (c n) -> p c n", c=4)
    WUB = arena[:, WUB_O:WUB_O + 4096].rearrange("p (c n) -> p c n", c=4)
    WOUT = arena[:, WOUT_O:WOUT_O + 8192].rearrange("p (c n) -> p c n", c=8)

    def KA(lo, hi):
        return ("arena", lo, hi)

    def uview(off, nbytes, dt):
        assert off % 4 == 0 and nbytes % 4 == 0 and off + nbytes <= UBYTES, (off, nbytes)
        ap = U[:, off // 4:(off + nbytes) // 4]
        if dt is BF16:
            ap = ap.bitcast(BF16)
        return ap, ("U", off, off + nbytes)

    st_pos = [0]

    def st_alloc(n):
        if st_pos[0] + n > 256:
            st_pos[0] = 0
        lo = st_pos[0]
        st_pos[0] += n
        return st[:, lo:lo + n], ("st", lo, lo + n)

    rr = {"n": 0}

    def alt(*engs):
        rr["n"] += 1
        return engs[rr["n"] % len(engs)]

    c32, c32k = uview(60 * KB, 1280 * 4, F32)
    sc.add("sp", lambda e: e.dma_start(out=c32, in_=cst_d), w=[c32k], dma=True)
    sc.add("dve", lambda e: e.tensor_copy(out=cstb, in_=c32), r=[c32k], w=["cstb"])
    sc.add("sp", lambda e: e.dma_start(out=g1_h[:], in_=g1_d), w=["g1"], dma=True)
    nh = nh_h[:]
    sc.add("pool", lambda e: e.memset(nh, -0.5), w=["nh"])
    sc.add("sp", lambda e: e.dma_start(out=fg, in_=fg_d), w=["fg"], dma=True)

    stage_i = [0]

    def load_weight_piece(src, n, dst, dk, stage_offs, scale_c=None, engs=("dve", "pool")):
        so = stage_offs[stage_i[0] % len(stage_offs)]
        stage_i[0] += 1
        if isinstance(so, tuple):
            eo = so[1]
            sv = arena[:, eo:eo + 2 * n].bitcast(F32)
            sk = KA(eo, eo + 2 * n)
        else:
            sv, sk = uview(so, n * 4, F32)
        sc.add("sp", lambda e: e.dma_start(out=sv, in_=src), w=[sk], dma=True)
        en = alt(*engs)
        if en == "act":
            if scale_c is not None:
                sc1 = g1_h[:, scale_c:scale_c + 1]
                sc.add("act", lambda e: e.activation(out=dst, in_=sv, func=AF.Copy, scale=sc1),
                       r=[sk, "g1"], w=[dk])
            else:
                sc.add("act", lambda e: e.activation(out=dst, in_=sv, func=AF.Copy), r=[sk], w=[dk])
        elif scale_c is not None:
            scl = g1_h[:, scale_c:scale_c + 1].to_broadcast([128, n])
            sc.add(en, lambda e: e.tensor_tensor(out=dst, in0=sv, in1=scl, op=ALU.mult),
                   r=[sk, "g1"], w=[dk])
        else:
            sc.add(en, lambda e: e.tensor_copy(out=dst, in_=sv), r=[sk], w=[dk])

    def rms_load(blk, xb, xbk):
        src = x_d[blk * 128:(blk + 1) * 128, :]
        sc.add("sp", lambda e: e.dma_start(out=xb, in_=src), w=[xbk], dma=True)

    def rms_head_a(xb, xbk, hb, hbk):
        ss, ssk = st_alloc(1)
        rs, rsk = st_alloc(1)
        sc.add("act", lambda e: e.activation(out=hb, in_=xb, func=AF.Square, accum_out=ss),
               r=[xbk], w=[hbk, ssk])
        sc.add("dve", lambda e: e.tensor_scalar(out=rs, in0=ss, scalar1=1.0 / D, scalar2=EPS,
                                                op0=ALU.mult, op1=ALU.add), r=[ssk], w=[rsk])
        sc.add("pool", lambda e: e.tensor_tensor(out=rs, in0=rs, in1=nh[:, 0:1], op=ALU.pow),
               r=[rsk, "nh"], w=[rsk])
        return rs, rsk

    def rms_head_b(xb, xbk, hb, hbk, rs, rsk):
        sc.add("act", lambda e: e.activation(out=hb, in_=xb, func=AF.Copy, scale=rs),
               r=[xbk, rsk], w=[hbk])

    def rms_head(blk, xb, xbk, hb, hbk, load=True):
        if load:
            rms_load(blk, xb, xbk)
        rs, rsk = rms_head_a(xb, xbk, hb, hbk)
        rms_head_b(xb, xbk, hb, hbk, rs, rsk)

    def transposes(hb, hbk, bank, dst, dstk):
        pst = psb[bank].bitcast(BF16).rearrange("p (c t) -> p c t", c=8)
        pk = PK(bank)

        def f(e):
            ins = None
            for c in range(8):
                ins = e.transpose(out=pst[:, c, :], in_=hb[:, c * 128:(c + 1) * 128], identity=ident)
            return ins
        sc.add("pe", f, r=[hbk, "cstb"], w=pk)
        en = alt("dve", "act")
        if en == "dve":
            sc.add("dve", lambda e: e.tensor_copy(out=dst, in_=pst), r=pk, w=[dstk])
        else:
            sc.add("act", lambda e: e.activation(out=dst, in_=pst, func=AF.Copy), r=pk, w=[dstk])

    def sub_key(k, i, n):
        lo, hi = k[1], k[2]
        step = (hi - lo) // n
        return (k[0], lo + i * step, lo + (i + 1) * step)

    W1f, W1k = uview(0, 32 * KB, BF16)
    W1 = W1f.rearrange("p (c n) -> p c n", c=8)
    xbs = [uview(32 * KB + 4 * KB * i, 4 * KB, F32) for i in range(2)]
    for blk_ in range(2):
        rms_load(blk_, xbs[blk_][0], xbs[blk_][1])
    hbs = [uview(40 * KB + 2 * KB * i, 2 * KB, BF16) for i in range(2)]
    hTs = [uview(44 * KB + 8 * KB * i, 8 * KB, BF16) for i in range(2)]

    mmb = [2, 3, 4, 5]
    mmi = [0]

    def next_bank():
        b = mmb[mmi[0] % len(mmb)]
        mmi[0] += 1
        return b

    def w1keys(col, n):
        return [("U", (c * 2048 + col) * 2, (c * 2048 + col + n) * 2) for c in range(8)]

    def p1_hT(t):
        hTf, hTk = hTs[t % 2]
        return hTf.rearrange("p (c t) -> p c t", c=8), hTk

    def p1_rms(t, b, load=True):
        blk = t * 4 + b
        xb, xbk = xbs[blk % 2]
        hb, hbk = hbs[blk % 2]
        rms_head(blk, xb, xbk, hb, hbk, load=load)

    def p1_T(t, b):
        blk = t * 4 + b
        hb, hbk = hbs[blk % 2]
        hT3, hTk = p1_hT(t)
        transposes(hb, hbk, blk % 2, hT3[:, :, b * 128:(b + 1) * 128], sub_key(hTk, b, 4))

    def p1_mm(t, m):
        hT3, hTk = p1_hT(t)
        t0 = t * 512
        bank = next_bank()
        ps = psb[bank]
        pk = PK(bank)
        if m < 12:
            oc = m
            col = oc * 128 if oc < 8 else 1536 + (oc - 8) * 128

            def f(e):
                ins = None
                for c in range(8):
                    ins = e.matmul(ps, lhsT=W1[:, c, col:col + 128], rhs=hT3[:, c, :],
                                   start=(c == 0), stop=(c == 7))
                return ins
            sc.add("pe", f, r=w1keys(col, 128) + [hTk], w=pk)
            if oc < 4:
                dst = qT[:, oc, t0:t0 + 512]
                dk = KA(QO + oc * S + t0, QO + oc * S + t0 + 512)
                sc.add("dve", lambda e: e.tensor_scalar(out=dst, in0=ps, scalar1=0.125, scalar2=None,
                                                        op0=ALU.mult), r=pk, w=[dk])
            elif oc < 8:
                dst = kT[:, oc - 4, t0:t0 + 512]
                dk = KA(KO + (oc - 4) * S + t0, KO + (oc - 4) * S + t0 + 512)
                sc.add("act", lambda e: e.activation(out=dst, in_=ps, func=AF.Copy), r=pk, w=[dk])
            else:
                dst = yaT[:, oc - 8, t0:t0 + 512]
                dk = ("yaT", (oc - 8) * S + t0, (oc - 8) * S + t0 + 512)
                sc.add("act", lambda e: e.activation(out=dst, in_=ps, func=AF.Silu), r=pk, w=[dk])
        else:
            b = m - 12
            blk = t * 4 + b

            def f(e):
                ins = None
                for c in range(8):
                    ins = e.matmul(ps, lhsT=hT3[:, c, b * 128:(b + 1) * 128], rhs=W1[:, c, 1024:1536],
                                   start=(c == 0), stop=(c == 7))
                return ins
            sc.add("pe", f, r=w1keys(1024, 512) + [hTk], w=pk)
            dst = vv[:, blk, :]
            dk = KA(VO + blk * 512, VO + (blk + 1) * 512)
            sc.add("dve", lambda e: e.tensor_copy(out=dst, in_=ps), r=pk, w=[dk])

    for b in range(4):
        p1_rms(0, b, load=(b >= 2))
        p1_T(0, b)
    for c0 in range(0, 2048, 1024):
        for c in range(8):
            load_weight_piece(win_d[c * 128:(c + 1) * 128, c0:c0 + 1024], 1024, W1[:, c, c0:c0 + 1024],
                              ("U", (c * 2048 + c0) * 2, (c * 2048 + c0 + 1024) * 2),
                              [64 * KB, 68 * KB] + [("arena", VO + 8192 + 2048 * j) for j in range(4)], scale_c=c)
    for t in range(NT):
        for m in range(16):
            if t + 1 < NT and m % 4 == 0:
                p1_rms(t + 1, m // 4)
            p1_mm(t, m)
            if t + 1 < NT and m % 4 == 3:
                p1_T(t + 1, m // 4)

    P3STAGE = [32 * KB, 36 * KB, 40 * KB, 44 * KB, 64 * KB, 68 * KB]

    def emit_p3_weights(part):
        p3engs = ("pool",) if part == 0 else ("pool", "dve", "act")
        def wua(c):
            load_weight_piece(wua_d[c * 128:(c + 1) * 128, :], 1024, WUA[:, c, :],
                              KA(WUA_O + c * 1024, WUA_O + (c + 1) * 1024), P3STAGE, engs=p3engs)

        def wub(c):
            load_weight_piece(wub_d[c * 128:(c + 1) * 128, :], 1024, WUB[:, c, :],
                              KA(WUB_O + c * 1024, WUB_O + (c + 1) * 1024), P3STAGE, engs=p3engs)

        def wo(c):
            load_weight_piece(wout_d[c * 128:(c + 1) * 128, :], 1024, WOUT[:, c, :],
                              KA(WOUT_O + c * 1024, WOUT_O + (c + 1) * 1024), P3STAGE, engs=p3engs)

        def w3(c):
            for c0 in range(0, 3584, 1024):
                n = min(1024, 3584 - c0)
                load_weight_piece(win_d[c * 128:(c + 1) * 128, 2048 + c0:2048 + c0 + n], n,
                                  W3c[c][:, c0:c0 + n],
                                  KA(W3OFF[c] + c0, W3OFF[c] + c0 + n), P3STAGE,
                                  scale_c=c, engs=p3engs)
        if part == 0:
            for c in range(6):
                w3(c)
        else:
            for c in (6, 7):
                w3(c)
            for c in range(4):
                wub(c)
            for c in range(4):
                wua(c)
            for c in range(8):
                wo(c)

    NE, NSP, ND, NW = 2, 3, 3, 3
    e_b = [uview(0 + 4 * KB * i, 4 * KB, F32) for i in range(NE)]
    sp_b = [uview(8 * KB + 2 * KB * i, 2 * KB, BF16) for i in range(NSP)]
    d_b = [uview(14 * KB + 2 * KB * i, 2 * KB, BF16) for i in range(ND)]
    w_b = [uview(20 * KB + 2 * KB * i, 2 * KB, BF16) for i in range(NW)]
    OBK = [6, 7]

    def h3(ap):
        return ap.rearrange("p (h t) -> p h t", h=2)
    LP3 = [h3(LP[j]) for j in range(3)]
    LPK = [PK(2 * j) + PK(2 * j + 1) for j in range(3)]

    pairs = []
    chain_id = 0
    for p in range(4):
        for qt in range(NT):
            nkb = 4 * (qt + 1)
            for i in range(nkb):
                pairs.append(dict(p=p, qt=qt, kb=nkb - 1 - i, i=i, nkb=nkb, chain=chain_id))
            chain_id += 1
    NP = len(pairs)
    for k, u in enumerate(pairs):
        u["k"] = k
        u["L"] = k % 3
        u["ob"] = OBK[u["chain"] % 2]

    def c0_of(i):
        return max(0, (3 - i) * 128)

    HP = (slice(0, 64), slice(64, 128))

    def s1(u):
        p, qt, kb = u["p"], u["qt"], u["kb"]
        c0 = c0_of(u["i"])
        diag = u["i"] <= 3
        j = u["L"]

        def f(e):
            ins = None
            for hh in range(2):
                ins = e.matmul(LP[j][:, hh * 512 + c0:(hh + 1) * 512], lhsT=kT[HP[hh], p, kb * 128:(kb + 1) * 128],
                               rhs=qT[HP[hh], p, qt * 512 + c0:(qt + 1) * 512], start=True, stop=(not diag))
            if diag:
                for hh in range(2):
                    ins = e.matmul(LP[j][:, hh * 512 + c0:hh * 512 + c0 + 128], lhsT=ident, rhs=NEGW[:, 384:512],
                                   start=False, stop=True)
            return ins
        sc.add("pe", f, r=[KA(KO + p * S + kb * 128, KO + p * S + (kb + 1) * 128),
                           KA(QO + p * S + qt * 512, QO + p * S + (qt + 1) * 512), "cstb"],
               w=LPK[j])

    def s2(u):
        c0 = c0_of(u["i"])
        L = LP3[u["L"]][:, :, c0:512]
        ev, ek = e_b[u["k"] % NE]
        evs = h3(ev)[:, :, c0:512]
        sc.add("act", lambda e: e.activation(out=evs, in_=L, func=AF.Exp), r=LPK[u["L"]], w=[ek])

    def s3(u):
        c0 = c0_of(u["i"])
        ev, ek = e_b[u["k"] % NE]
        sv, sk = sp_b[u["k"] % NSP]
        evs, svs = h3(ev)[:, :, c0:512], h3(sv)[:, :, c0:512]
        sc.add("act", lambda e: e.activation(out=svs, in_=evs, func=AF.Ln, bias=1.0), r=[ek], w=[sk])

    def dbuf(u):
        if u["i"] == 0:
            return None
        if u["i"] == 1:
            v_, k_ = sp_b[(u["k"] - 1) % NSP]
        else:
            v_, k_ = d_b[u["k"] % ND]
        return v_, k_, c0_of(u["i"] - 1)

    def s4d(u):
        if u["i"] < 2:
            return
        prev = pairs[u["k"] - 1]
        dpv, dpk, cdp = dbuf(prev)
        spv, spk = sp_b[prev["k"] % NSP]
        csp = c0_of(prev["i"])
        dv, dk = d_b[u["k"] % ND]
        d3, dp3, sp3 = h3(dv), h3(dpv), h3(spv)
        if csp < cdp:
            sc.add("dve", lambda e: e.tensor_copy(out=d3[:, :, csp:cdp], in_=sp3[:, :, csp:cdp]), r=[spk], w=[dk])
        sc.add("dve", lambda e: e.tensor_tensor(out=d3[:, :, cdp:512], in0=dp3[:, :, cdp:512], in1=sp3[:, :, cdp:512],
                                                op=ALU.add), r=[dpk, spk], w=[dk])

    def s4(u):
        c0 = c0_of(u["i"])
        j = u["L"]
        sv, sk = sp_b[u["k"] % NSP]
        sv3 = h3(sv)
        dd = dbuf(u)

        def f(e):
            ins = None
            for hh in range(2):
                ins = e.matmul(LP[j][:, hh * 512 + c0:(hh + 1) * 512], lhsT=nUi, rhs=sv3[:, hh, c0:512],
                               start=False, stop=(dd is None), skip_group_check=True)
            if dd is not None:
                cd = dd[2]
                dd3 = h3(dd[0])
                for hh in range(2):
                    ins = e.matmul(LP[j][:, hh * 512 + cd:(hh + 1) * 512], lhsT=nOnes, rhs=dd3[:, hh, cd:512],
                                   start=False, stop=True, skip_group_check=True)
            return ins
        rk = [sk, "cstb"] + ([dd[1]] if dd is not None else [])
        sc.add("pe", f, r=rk + LPK[j], w=LPK[j])

    def s5(u):
        c0 = c0_of(u["i"])
        L = LP3[u["L"]][:, :, c0:512]
        wv, wk = w_b[u["k"] % NW]
        w3 = h3(wv)
        if c0 > 0:
            sc.add("pool", lambda e: e.memset(w3[:, :, 0:c0], 0.0), w=[wk])
        sc.add("act", lambda e: e.activation(out=w3[:, :, c0:512], in_=L, func=AF.Exp), r=LPK[u["L"]], w=[wk])

    def s6(u):
        p, qt, kb = u["p"], u["qt"], u["kb"]
        O = psb[u["ob"]]
        wv, wk = w_b[u["k"] % NW]
        w3 = h3(wv)
        ok = PK(u["ob"])
        first, last = (u["i"] == 0), (u["i"] == u["nkb"] - 1)

        def f(e):
            ins = None
            for hh in range(2):
                h = 2 * p + hh
                ins = e.matmul(O[HP[hh], :], lhsT=vv[:, kb, h * 64:(h + 1) * 64], rhs=w3[:, hh, :],
                               start=first, stop=last)
            return ins
        sc.add("pe", f, r=[wk, KA(VO + kb * 512, VO + (kb + 1) * 512)] + ([] if first else ok), w=ok)
        if last:
            dst = yaT[:, p, qt * 512:(qt + 1) * 512]
            dk = ("yaT", p * S + qt * 512, p * S + (qt + 1) * 512)
            sc.add("dve", lambda e: e.tensor_tensor(out=dst, in0=O, in1=dst, op=ALU.mult),
                   r=ok + [dk], w=[dk])

    def P_(jx):
        return pairs[jx] if 0 <= jx < NP else None

    for step in range(-1, NP + 3):
        pm2, pm1, pc, pn = P_(step - 2), P_(step - 1), P_(step), P_(step + 1)
        if pm1:
            s4(pm1)
        if pc:
            s2(pc)
        if pm2:
            s5(pm2)
        if pc:
            s3(pc)
        if pn:
            s1(pn)
        if pm2:
            s6(pm2)
        if pn:
            s4d(pn)
    emit_p3_weights(0)

    lgv, lgk = uview(8 * KB, 2 * KB, F32)
    lbv, lbk = uview(10 * KB, 2 * KB, F32)
    bsv, bsk = uview(12 * KB, 2 * KB, F32)
    wsb, wsbk = uview(14 * KB, 2 * KB, BF16)
    lgT = lgv[:, 0:4]
    lbT = lgv[:, 4:8]
    negb3 = lbv.rearrange("p (c t) -> p c t", c=4)
    negbk = lbk
    sc.add("sp", lambda e: e.dma_start(out=lgv[:, 0:8], in_=lg_d), w=[lgk], dma=True)
    sc.add("sp", lambda e: e.dma_start(out=bsv, in_=bsT_d), w=[bsk], dma=True)
    ws32, ws32k = uview(0, 4 * KB, F32)
    mk32, mk32k = uview(4 * KB, 4 * KB, F32)
    sc.add("sp", lambda e: e.dma_start(out=ws32, in_=wsT_d), w=[ws32k], dma=True)
    sc.add("sp", lambda e: e.dma_start(out=mk32, in_=msk_d), w=[mk32k], dma=True)
    sc.add("pool", lambda e: e.tensor_tensor(out=wsb, in0=ws32, in1=mk32, op=ALU.mult),
           r=[ws32k, mk32k], w=[wsbk])
    wsT3 = wsb.rearrange("p (g t) -> p g t", g=8)
    bs3 = bsv.rearrange("p (c t) -> p c t", c=4)

    def f_rowsum(e):
        ins = None
        for g in range(8):
            fc_, gg = g // 2, g % 2
            ins = e.matmul(psb[3][gg * 64:(gg + 1) * 64, fc_ * 128:(fc_ + 1) * 128],
                           lhsT=nOnes[:, 0:64], rhs=wsT3[:, g, :], start=True, stop=True)
        return ins
    sc.add("pe", f_rowsum, r=[wsbk, "cstb"], w=PK(3))
    for fc_ in range(4):
        sc.add("dve", lambda e, fc_=fc_: e.scalar_tensor_tensor(
            out=negb3[:, fc_, :], in0=psb[3][:, fc_ * 128:(fc_ + 1) * 128], scalar=lbT[:, fc_:fc_ + 1],
            in1=bs3[:, fc_, :], op0=ALU.mult, op1=ALU.subtract),
            r=PK(3) + [lgk, bsk], w=[negbk])

    xb3 = [uview(16 * KB + 4 * KB * i, 4 * KB, F32) for i in range(2)]
    hb3 = [uview(24 * KB + 2 * KB * i, 2 * KB, BF16) for i in range(2)]
    hT3f, hT3k = uview(28 * KB, 4 * KB, BF16)
    hTt = hT3f.rearrange("p (c t) -> p c t", c=8)
    tmpA = [uview(32 * KB + 2 * KB * i, 2 * KB, F32) for i in range(2)]
    gvb = [uview(36 * KB + 2 * KB * i, 2 * KB, F32) for i in range(2)]
    vnf, vnk = uview(40 * KB, 2 * KB, BF16)
    vn3 = vnf.rearrange("p (b f) -> p b f", b=2)
    szb = [uview(42 * KB + 512 * i, 512, BF16) for i in range(2)]
    gsf, gsk = uview(44 * KB, 2 * KB, BF16)
    gs3 = gsf.rearrange("p (c t) -> p c t", c=4)
    ybf, ybk = uview(46 * KB, 2 * KB, BF16)
    yb3 = ybf.rearrange("p (c t) -> p c t", c=4)
    taf, tak = uview(48 * KB, 4 * KB, BF16)
    ta3 = taf.rearrange("p (c t) -> p c t", c=8)
    tbf, tbk = uview(52 * KB, 4 * KB, BF16)
    tb3 = tbf.rearrange("p (c t) -> p c t", c=8)
    m1b = [uview(56 * KB + KB * i, KB, F32) for i in range(2)]
    m2b = [uview(58 * KB + KB * i, KB, F32) for i in range(2)]
    mTf, mTk = uview(60 * KB, 4 * KB, BF16)
    mT3 = mTf.rearrange("p (c t) -> p c t", c=8)
    xrb = [uview(64 * KB + 4 * KB * i, 4 * KB, F32) for i in range(2)]

    HB_B = (4, 5, 6, 7, 1, 2)
    HB_CD = (4, 5, 6, 7)
    hb_state = {"lst": HB_B, "i": 0}

    def set_rotation(lst):
        hb_state["lst"], hb_state["i"] = lst, 0

    def next_half():
        lst = hb_state["lst"]
        bk = lst[hb_state["i"] % len(lst)]
        hb_state["i"] += 1
        return psb[bk][:, 0:256], PK(bk)

    def gelu_p1(ps, pk, tA, tAk, out, outk):
        sc.add("act", lambda e: e.activation(out=out, in_=ps, func=AF.Copy), r=pk, w=[outk])
        sc.add("act", lambda e: e.activation(out=tA, in_=ps, func=AF.Square), r=pk, w=[tAk])

    def gelu_p2(tA, tAk, out, outk):
        sc.add("dve", lambda e: e.tensor_scalar(out=tA, in0=tA, scalar1=C_G1 * C_G2, scalar2=C_G1,
                                                op0=ALU.mult, op1=ALU.add), r=[tAk], w=[tAk])
        sc.add("dve", lambda e: e.tensor_tensor(out=tA, in0=tA, in1=out, op=ALU.mult), r=[tAk, outk], w=[tAk])
        sc.add("act", lambda e: e.activation(out=tA, in_=tA, func=AF.Tanh), r=[tAk], w=[tAk])
        sc.add("dve", lambda e: e.scalar_tensor_tensor(out=out, in0=tA, scalar=1.0, in1=out,
                                                       op0=ALU.add, op1=ALU.mult), r=[tAk, outk], w=[outk])

    out_dmas = []
    _order = list(range(NT3))

    def p3_head_dma(t):
        for b in range(2):
            blk = t * 2 + b
            xb, xbk = xb3[blk % 2]
            rms_load(blk, xb, xbk)

    def p3_head_rms(t):
        for b in range(2):
            blk = t * 2 + b
            xb, xbk = xb3[blk % 2]
            hb, hbk = hb3[blk % 2]
            rms_head(blk, xb, xbk, hb, hbk, load=False)

    def p3_head_rms_a(t):
        out = []
        for b in range(2):
            blk = t * 2 + b
            xb, xbk = xb3[blk % 2]
            hb, hbk = hb3[blk % 2]
            out.append(rms_head_a(xb, xbk, hb, hbk))
        return out

    def p3_head_rms_b(t, rss):
        for b in range(2):
            blk = t * 2 + b
            xb, xbk = xb3[blk % 2]
            hb, hbk = hb3[blk % 2]
            rms_head_b(xb, xbk, hb, hbk, rss[b][0], rss[b][1])

    def p3_head_T(t):
        for b in range(2):
            blk = t * 2 + b
            hb, hbk = hb3[blk % 2]
            transposes(hb, hbk, 0, hTt[:, :, b * 128:(b + 1) * 128], sub_key(hT3k, b, 2))

    GATES = [(w_, oc_) for w_ in (0, 1) for oc_ in range(8)]

    def emit_gates(lst):
        for which, oc in lst:
            t3, tk, coff = ((ta3, tak, 1536), (tb3, tbk, 2560))[which]
            ps, pk = next_half()

            def fg_(e, oc=oc, ps=ps, coff=coff):
                ins = None
                for c in range(8):
                    ins = e.matmul(ps, lhsT=W3c[c][:, coff + oc * 128:coff + (oc + 1) * 128], rhs=hTt[:, c, :],
                                   start=(c == 0), stop=(c == 7))
                return ins
            sc.add("pe", fg_, r=[hT3k] + W3KEYS, w=pk)
            dst = t3[:, oc, :]
            sc.add("act", lambda e, dst=dst, ps=ps: e.activation(out=dst, in_=ps, func=AF.Tanh, scale=0.5),
                   r=pk, w=[sub_key(tk, oc, 8)])

    def p3_B(t, pre_gates_hook=None, mid_gates_hook=None, after_vb_hook=None):
        tok0 = t * 256
        set_rotation(HB_B)
        vchains = []
        for b in range(2):
            bank = 1 + b
            ps = psb[bank]
            pk = PK(bank)

            def f(e, b=b, ps=ps):
                ins = None
                for c in range(8):
                    ins = e.matmul(ps, lhsT=hTt[:, c, b * 128:(b + 1) * 128], rhs=W3c[c][:, 512:1024],
                                   start=(c == 0), stop=(c == 7))
                return ins
            sc.add("pe", f, r=[hT3k] + W3KEYS, w=pk)
            tA, tAk = tmpA[b]
            gv, gvk = gvb[b]
            gelu_p1(ps, pk, tA, tAk, gv, gvk)
            sc.capture()
            gelu_p2(tA, tAk, gv, gvk)
            vchains.append(sc.end_capture())
        if after_vb_hook is not None:
            after_vb_hook()

        def ln_chain(b):
            tA, tAk = tmpA[b]
            gv, gvk = gvb[b]
            gv3 = gv.rearrange("p (g c) -> p g c", g=8)
            tA3 = tA.rearrange("p (g c) -> p g c", g=8)
            s1_, s1k = st_alloc(8)
            s2_, s2k = st_alloc(8)
            qq, qqk = st_alloc(8)
            vr, vrk = st_alloc(8)
            s1b = s1_.unsqueeze(2).to_broadcast([128, 8, 64])
            vrb = vr.unsqueeze(2).to_broadcast([128, 8, 64])
            vdst = vn3[:, b, :].rearrange("p (g c) -> p g c", g=8)
            sc.add("dve", lambda e: e.tensor_reduce(out=s1_, in_=gv3, axis=AX.X, op=ALU.add), r=[gvk], w=[s1k])
            sc.add("act", lambda e: e.activation(out=tA, in_=gv, func=AF.Square), r=[gvk], w=[tAk])
            sc.add("dve", lambda e: e.tensor_reduce(out=s2_, in_=tA3, axis=AX.X, op=ALU.add), r=[tAk], w=[s2k])
            sc.add("dve", lambda e: e.scalar_tensor_tensor(out=gv3, in0=gv3, scalar=64.0, in1=s1b,
                                                           op0=ALU.mult, op1=ALU.subtract), r=[gvk, s1k], w=[gvk])
            sc.add("dve", lambda e: e.tensor_tensor(out=qq, in0=s1_, in1=s1_, op=ALU.mult), r=[s1k], w=[qqk])
            sc.add("dve", lambda e: e.scalar_tensor_tensor(out=vr, in0=s2_, scalar=64.0, in1=qq,
                                                           op0=ALU.mult, op1=ALU.subtract), r=[s2k, qqk], w=[vrk])
            sc.add("dve", lambda e: e.tensor_scalar(out=vr, in0=vr, scalar1=4096.0 * 4.0 * EPS, scalar2=None,
                                                    op0=ALU.add), r=[vrk], w=[vrk])
            sc.add("pool", lambda e: e.tensor_tensor(out=vr, in0=vr, in1=nh, op=ALU.pow), r=[vrk, "nh"], w=[vrk])
            sc.add("dve", lambda e: e.tensor_tensor(out=vdst, in0=gv3, in1=vrb, op=ALU.mult),
                   r=[gvk, vrk], w=[sub_key(vnk, b, 2)])
        for fc0 in (0, 2):
            uch = []
            pend = []
            for fc in (fc0, fc0 + 1):
                psu, pku = next_half()
                psz, pkz = next_half()

                def fu(e, fc=fc, psu=psu):
                    ins = None
                    for c in range(8):
                        ins = e.matmul(psu, lhsT=W3c[c][:, fc * 128:(fc + 1) * 128], rhs=hTt[:, c, :],
                                       start=(c == 0), stop=(c == 7))
                    return ins

                def fz(e, fc=fc, psz=psz):
                    ins = None
                    for c in range(8):
                        ins = e.matmul(psz, lhsT=W3c[c][:, 1024 + fc * 128:1024 + (fc + 1) * 128], rhs=hTt[:, c, :],
                                       start=(c == 0), stop=(c == 7))
                    return ins
                sc.add("pe", fu, r=[hT3k] + W3KEYS, w=pku)
                sc.add("pe", fz, r=[hT3k] + W3KEYS, w=pkz)
                pend.append((fc, psu, pku, psz, pkz))
            for fc, psu, pku, psz, pkz in pend:
                tA, tAk = m2b[fc % 2]
                gu, guk = m1b[fc % 2]
                sz, szk = szb[fc % 2]
                gdst = gs3[:, fc, :]
                gelu_p1(psu, pku, tA, tAk, gu, guk)
                sc.add("act", lambda e, sz=sz, psz=psz: e.activation(out=sz, in_=psz, func=AF.Silu), r=pkz, w=[szk])
                sc.capture()
                gelu_p2(tA, tAk, gu, guk)
                sc.add("pool", lambda e, gdst=gdst, gu=gu, sz=sz: e.tensor_tensor(out=gdst, in0=gu, in1=sz, op=ALU.mult),
                       r=[guk, szk], w=[sub_key(gsk, fc, 4)])
                uch.append(sc.end_capture())
            emit_gates(GATES[0:2] if fc0 == 0 else GATES[2:4])
            if fc0 == 0:
                sc.interleave(vchains + uch)
            else:
                lch = []
                for b in range(2):
                    sc.capture()
                    ln_chain(b)
                    lch.append(sc.end_capture())
                sc.interleave(lch)
                sc.interleave(uch)
        if pre_gates_hook is not None:
            pre_gates_hook()
        emit_gates(GATES[4:8])
        if mid_gates_hook is not None:
            mid_gates_hook()
        emit_gates(GATES[8:14])

    def p3_CD(t, mid_hook=None):
        tok0 = t * 256
        set_rotation(HB_CD)
        for fc in range(4):
            ps, pk = next_half()

            def fs(e, fc=fc, ps=ps):
                ins = None
                for b in range(2):
                    for gg in range(2):
                        g = 2 * fc + gg
                        ins = e.matmul(ps[gg * 64:(gg + 1) * 64, b * 128:(b + 1) * 128],
                                       lhsT=vn3[:, b, g * 64:(g + 1) * 64], rhs=wsT3[:, g, :],
                                       start=True, stop=True)
                return ins
            sc.add("pe", fs, r=[vnk, wsbk], w=pk)
            m1, m1k = (m1b + m2b)[fc]
            nbb = negb3[:, fc, :].unsqueeze(1).to_broadcast([128, 2, 128])
            ps3 = ps.rearrange("p (b t) -> p b t", b=2)
            m13 = m1.rearrange("p (b t) -> p b t", b=2)
            lgc = lgT[:, fc:fc + 1]
            sc.add("dve", lambda e, m13=m13, ps3=ps3, nbb=nbb, lgc=lgc: e.scalar_tensor_tensor(
                out=m13, in0=ps3, scalar=lgc, in1=nbb, op0=ALU.mult, op1=ALU.subtract),
                r=pk + [negbk, lgk], w=[m1k])
        for fc in range(4):
            m1, m1k = (m1b + m2b)[fc]
            ydst = yb3[:, fc, :]
            gsrc = gs3[:, fc, :]
            sc.add("dve", lambda e, ydst=ydst, m1=m1, gsrc=gsrc: e.scalar_tensor_tensor(
                out=ydst, in0=m1, scalar=0.5, in1=gsrc, op0=ALU.mult, op1=ALU.mult),
                r=[m1k, sub_key(gsk, fc, 4)], w=[sub_key(ybk, fc, 4)])
        emit_gates(GATES[14:16])
        if mid_hook is not None:
            mid_hook()
        for oc in range(8):
            psa, pka = next_half()
            psb_, pkb = next_half()

            def fa(e, oc=oc, psa=psa, tok0=tok0):
                ins = None
                for c in range(4):
                    ins = e.matmul(psa, lhsT=WUA[:, c, oc * 128:(oc + 1) * 128], rhs=yaT[:, c, tok0:tok0 + 256],
                                   start=(c == 0), stop=(c == 3))
                return ins

            def fb(e, oc=oc, psb_=psb_):
                ins = None
                for c in range(4):
                    ins = e.matmul(psb_, lhsT=WUB[:, c, oc * 128:(oc + 1) * 128], rhs=yb3[:, c, :],
                                   start=(c == 0), stop=(c == 3))
                return ins
            sc.add("pe", fa, r=[KA(WUA_O, WUA_O + 4096)] + [("yaT", c * S + tok0, c * S + tok0 + 256) for c in range(4)], w=pka)
            if oc == 0:
                for c in range(4):
                    sc.add("pe", lambda e, c=c, psb_=psb_: e.matmul(
                        psb_, lhsT=WUB[:, c, 0:128], rhs=yb3[:, c, :], start=(c == 0), stop=(c == 3)),
                        r=[KA(WUB_O, WUB_O + 4096), sub_key(ybk, c, 4)], w=pkb)
            else:
                sc.add("pe", fb, r=[KA(WUB_O, WUB_O + 4096), ybk], w=pkb)
            m1, m1k = m1b[oc % 2]
            m2, m2k = m2b[oc % 2]
            tas = ta3[:, oc, :]
            tbs = tb3[:, oc, :]
            sc.add("dve", lambda e, m1=m1, tas=tas, psa=psa: e.scalar_tensor_tensor(
                out=m1, in0=tas, scalar=1.0, in1=psa, op0=ALU.add, op1=ALU.mult),
                r=pka + [sub_key(tak, oc, 8)], w=[m1k])
            sc.add("dve", lambda e, m2=m2, tbs=tbs, psb_=psb_: e.scalar_tensor_tensor(
                out=m2, in0=tbs, scalar=1.0, in1=psb_, op0=ALU.add, op1=ALU.mult),
                r=pkb + [sub_key(tbk, oc, 8)], w=[m2k])
            mdst = mT3[:, oc, :]
            sc.add("pool", lambda e, mdst=mdst, m1=m1, m2=m2: e.tensor_tensor(out=mdst, in0=m1, in1=m2, op=ALU.add),
                   r=[m1k, m2k], w=[sub_key(mTk, oc, 8)])
    def p3_E(t):
        tok0 = t * 256
        for b in range(2):
            blk = t * 2 + b
            xr, xrk = xrb[blk % 2]
            src = x_d[blk * 128:(blk + 1) * 128, :]
            sc.add("sp", lambda e, xr=xr, src=src: e.dma_start(out=xr, in_=src), w=[xrk], dma=True)
            for half in range(2):
                bank = (3, 0)[half]
                ps = psb[bank]
                pk = PK(bank)

                def fo(e, b=b, half=half, ps=ps):
                    ins = None
                    for c in range(8):
                        ins = e.matmul(ps, lhsT=mT3[:, c, b * 128:(b + 1) * 128],
                                       rhs=WOUT[:, c, half * 512:(half + 1) * 512],
                                       start=(c == 0), stop=(c == 7))
                    return ins
                sc.add("pe", fo, r=[mTk, KA(WOUT_O, WOUT_O + 8192)], w=pk)
                xh = xr[:, half * 512:(half + 1) * 512]
                sc.add("dve", lambda e, xh=xh, ps=ps: e.scalar_tensor_tensor(
                    out=xh, in0=ps, scalar=0.5, in1=xh, op0=ALU.mult, op1=ALU.add),
                    r=pk + [xrk], w=[xrk])

    def p3_E_fin(t):
        for b in range(2):
            blk = t * 2 + b
            xr, xrk = xrb[blk % 2]
            ss, ssk = st_alloc(1)
            rs, rsk = st_alloc(1)
            junk = mTf[:, 0:1024]
            sc.add("act", lambda e, junk=junk, xr=xr, ss=ss: e.activation(out=junk, in_=xr, func=AF.Square, accum_out=ss),
                   r=[xrk], w=[mTk, ssk])
            sc.add("dve", lambda e, rs=rs, ss=ss: e.tensor_scalar(out=rs, in0=ss, scalar1=1.0 / D, scalar2=EPS,
                                                                  op0=ALU.mult, op1=ALU.add), r=[ssk], w=[rsk])
            sc.add("pool", lambda e, rs=rs: e.tensor_tensor(out=rs, in0=rs, in1=nh[:, 0:1], op=ALU.pow),
                   r=[rsk, "nh"], w=[rsk])
            sc.add("dve", lambda e, xr=xr, rs=rs: e.scalar_tensor_tensor(
                out=xr, in0=xr, scalar=rs, in1=fg, op0=ALU.mult, op1=ALU.mult),
                r=[xrk, rsk, "fg"], w=[xrk])
            dst = out_d[blk * 128:(blk + 1) * 128, :]
            od = sc.add("sp", lambda e, xr=xr, dst=dst: e.dma_start(out=dst, in_=xr), r=[xrk], w=[], dma=True)
            out_dmas.append(od)

    p3_head_dma(_order[0])
    p3_head_rms(_order[0])
    p3_head_T(_order[0])
    emit_p3_weights(1)
    for _ti, t in enumerate(_order):
        nxt = _order[_ti + 1] if _ti + 1 < len(_order) else None
        _rss = []

        def _next_head_a(nxt=nxt, _rss=_rss):
            p3_head_dma(nxt)
            _rss.extend(p3_head_rms_a(nxt))

        def _next_head_b(nxt=nxt, _rss=_rss):
            p3_head_rms_b(nxt, _rss)
        prev = _order[_ti - 1] if _ti > 0 else None
        p3_B(t, pre_gates_hook=_next_head_a if nxt is not None else None,
             mid_gates_hook=_next_head_b if nxt is not None else None,
             after_vb_hook=(lambda prev=prev: p3_E_fin(prev)) if prev is not None else None)
        p3_CD(t, mid_hook=(lambda nxt=nxt: p3_head_T(nxt)) if nxt is not None else None)
        p3_E(t)
        if nxt is None:
            p3_E_fin(t)

    sc.add("sp", lambda e: None, extra=out_dmas)

    sc.finalize()
    with nc.Block() as block:
        @block.sync
        def _(eng):
            sc.emit("sp", eng)

        @block.tensor
        def _(eng):
            sc.emit("pe", eng)

        @block.scalar
        def _(eng):
            sc.emit("act", eng)

        @block.vector
        def _(eng):
            sc.emit("dve", eng)

        @block.gpsimd
        def _(eng):
            sc.emit("pool", eng)
    return nc


def _consts():
    c = np.zeros((128, 1280), np.float32)
    j = np.arange(128)[:, None]
    s = np.arange(128)[None, :]
    c[:, 0:128] = np.eye(128, dtype=np.float32)
    c[:, 128:256] = np.where(j >= s, -1.0, 0.0)
    c[:, 256:384] = -1.0
    jj = np.arange(896)[None, :]
    c[:, 384:1280] = np.where((jj - 384) <= j, NEG, 0.0)
    m = np.zeros((128, 8, 128), np.float32)
    sblk = (np.arange(128) // 64)[:, None]
    tblk = (np.arange(128) // 64)[None, :]
    m[:] = np.where(sblk <= tblk, 1.0, 0.0)[:, None, :]
    return c, m.reshape(128, 1024)


_NC_CACHE = {}


def kernel(x, norm_g, w_in, sgu_ln_g, sgu_ln_b, w_spatial, b_spatial, w_up_a, w_up_b, w_out, final_norm_g):
    x = np.asarray(x, np.float32)
    B, S, _ = x.shape
    f32 = lambda a: np.ascontiguousarray(np.asarray(a, np.float32))
    cst, msk = _consts()
    g1 = f32(np.asarray(norm_g)[0].reshape(8, 128).T)
    fg = f32(np.broadcast_to(np.asarray(final_norm_g).reshape(1, D), (128, D)))
    lg = f32(np.concatenate([np.asarray(sgu_ln_g)[0].reshape(4, 128).T,
                             np.asarray(sgu_ln_b)[0].reshape(4, 128).T], axis=1))
    bs = np.asarray(b_spatial)[0]
    bsT = np.empty((128, 4, 128), np.float32)
    for g in range(8):
        bsT[(g % 2) * 64:(g % 2) * 64 + 64, g // 2, :] = bs[g][None, :]
    wsT = f32(np.transpose(np.asarray(w_spatial)[0], (2, 0, 1)).reshape(128, 1024))
    common = {
        "w_in": f32(np.asarray(w_in)[0]), "w_up_a": f32(np.asarray(w_up_a)[0]),
        "w_up_b": f32(np.asarray(w_up_b)[0]), "w_out": f32(np.asarray(w_out)[0]),
        "g1": g1, "fg": fg, "lg": lg, "bsT": f32(bsT.reshape(128, 512)),
        "wsT": wsT, "msk": f32(msk), "cst": f32(cst),
    }
    if S not in _NC_CACHE:
        _NC_CACHE[S] = build_nc(S)
    nc = _NC_CACHE[S]
    in_maps = []
    for b in range(B):
        m = dict(common)
        m["x"] = f32(x[b])
        in_maps.append(m)
    res = run_bass_kernel_spmd(nc, in_maps, core_ids=list(range(B)))
    return np.stack([np.asarray(r["out"], np.float32).reshape(S, D) for r in res.results], axis=0)
```

```python
import numpy as np
import concourse.bass as bass
import concourse.mybir as mybir
from concourse.bass_utils import run_bass_kernel_spmd

F32 = mybir.dt.float32
BF16 = mybir.dt.bfloat16
AF = mybir.ActivationFunctionType
ALU = mybir.AluOpType
AX = mybir.AxisListType

D = 1024
DIN = 5632
EPS = 1e-6
NEG = -30000.0
C_G1 = 0.7978845608028654
C_G2 = 0.044715
N_DMA_SEMS = 16
INF = 1 << 60
KB = 1024


class Op:
    __slots__ = ("eng", "fn", "deps", "dma", "signal", "val", "sem", "prev_val", "waits", "idx")


class Sched:
    ENGS = ("pe", "act", "dve", "pool", "sp")

    def __init__(self, nc):
        self.nc = nc
        self.q = {e: [] for e in self.ENGS}
        self.all = []
        self.acc = {}
        self.esem = {}
        self.dsems = []
        self._cap = None

    @staticmethod
    def _norm(k):
        if isinstance(k, tuple):
            return k[0], k[1], k[2]
        return k, 0, INF

    def capture(self):
        self._cap = []

    def end_capture(self):
        c, self._cap = self._cap, None
        return c

    def interleave(self, chains):
        idx = [0] * len(chains)
        left = sum(len(c) for c in chains)
        while left:
            for i, c in enumerate(chains):
                if idx[i] < len(c):
                    self.add(*c[idx[i]])
                    idx[i] += 1
                    left -= 1

    def add(self, eng, fn, r=(), w=(), extra=(), dma=False):
        if self._cap is not None:
            self._cap.append((eng, fn, tuple(r), tuple(w), tuple(extra), dma))
            return None
        op = Op()
        op.eng, op.fn, op.dma = eng, fn, dma
        op.signal, op.val, op.sem, op.prev_val, op.waits = False, 0, None, 0, []
        op.idx = len(self.all)
        deps = set(e for e in extra if e is not None)
        rn = [self._norm(k) for k in r]
        wn = [self._norm(k) for k in w]
        for name, lo, hi in rn:
            for rec in self.acc.get(name, ()):
                if rec[3] and rec[0] < hi and lo < rec[1]:
                    deps.add(rec[2])
        for name, lo, hi in wn:
            for rec in self.acc.get(name, ()):
                if rec[0] < hi and lo < rec[1]:
                    deps.add(rec[2])
        for name, lo, hi in rn:
            self.acc.setdefault(name, []).append([lo, hi, op, False])
        for name, lo, hi in wn:
            lst = self.acc.setdefault(name, [])
            lst[:] = [rec for rec in lst if not (lo <= rec[0] and rec[1] <= hi)]
            lst.append([lo, hi, op, True])
        deps.discard(op)
        op.deps = deps
        self.q[eng].append(op)
        self.all.append(op)
        return op

    def finalize(self):
        nc = self.nc
        for e in self.ENGS:
            self.esem[e] = nc.alloc_semaphore("sem_" + e)
        self.dsems = [nc.alloc_semaphore("dsem%d" % i) for i in range(N_DMA_SEMS)]
        for op in self.all:
            for d in op.deps:
                if d.dma:
                    continue
                if d.eng == op.eng and d.eng == "pe":
                    continue
                d.signal = True
        uses = [0] * N_DMA_SEMS
        ndma = 0
        for e in self.ENGS:
            cnt = 0
            for op in self.q[e]:
                if op.dma:
                    i = ndma % N_DMA_SEMS
                    ndma += 1
                    op.sem = self.dsems[i]
                    op.prev_val = 16 * uses[i]
                    uses[i] += 1
                    op.val = 16 * uses[i]
                elif op.signal:
                    cnt += 1
                    op.val = cnt
        for e in self.ENGS:
            waited = {}
            for op in self.q[e]:
                need = {}
                for d in op.deps:
                    if d.dma:
                        s, v = d.sem, d.val
                    else:
                        if d.eng == e and e == "pe":
                            continue
                        s, v = self.esem[d.eng], d.val
                    if need.get(s.num, (None, 0))[1] < v:
                        need[s.num] = (s, v)
                if op.dma and op.prev_val > 0:
                    if need.get(op.sem.num, (None, 0))[1] < op.prev_val:
                        need[op.sem.num] = (op.sem, op.prev_val)
                op.waits = []
                for num, (s, v) in need.items():
                    if waited.get(num, 0) < v:
                        op.waits.append((s, v))
                        waited[num] = v

    def emit(self, e, eng):
        for op in self.q[e]:
            for (s, v) in op.waits:
                eng.wait_ge(s, v)
            ins = op.fn(eng)
            if op.dma:
                ins.then_inc(op.sem, 16)
            elif op.signal:
                ins.then_inc(self.esem[e], 1)


def PK(bank, ph=None, c0=0, c1=512):
    name = "ps%d" % bank
    halves = (0, 1) if ph is None else (ph,)
    return [(name, h * 1024 + c0, h * 1024 + c1) for h in halves]


def build_nc(S):
    NB = S // 128
    NT = S // 512
    NT3 = S // 256
    nc = bass.Bass("TRN2", target_bir_lowering=False)

    def din(name, shape):
        return nc.dram_tensor(name, list(shape), F32, kind="ExternalInput").ap()

    x_d = din("x", (S, D))
    win_d = din("w_in", (D, DIN))
    wua_d = din("w_up_a", (512, D))
    wub_d = din("w_up_b", (512, D))
    wout_d = din("w_out", (D, D))
    g1_d = din("g1", (128, 8))
    fg_d = din("fg", (128, D))
    lg_d = din("lg", (128, 8))
    bsT_d = din("bsT", (128, 512))
    wsT_d = din("wsT", (128, 1024))
    msk_d = din("msk", (128, 1024))
    cst_d = din("cst", (128, 1280))
    out_d = nc.dram_tensor("out", [S, D], F32, kind="ExternalOutput").ap()

    sc = Sched(nc)

    ARENA_E = 49152
    UBYTES = 72 * KB
    arena_h = nc.alloc_sbuf_tensor("arena", [128, ARENA_E], BF16)
    ya_h = nc.alloc_sbuf_tensor("yaT", [128, 4 * S], BF16)
    cstb_h = nc.alloc_sbuf_tensor("cstb", [128, 1280], BF16)
    g1_h = nc.alloc_sbuf_tensor("g1s", [128, 8], F32)
    fg_h = nc.alloc_sbuf_tensor("fgs", [128, D], F32)
    st_h = nc.alloc_sbuf_tensor("st", [128, 256], F32)
    nh_h = nc.alloc_sbuf_tensor("nh", [128, 8], F32)
    U_h = nc.alloc_sbuf_tensor("U", [128, UBYTES // 4], F32)
    LP = [nc.alloc_psum_tensor("lp%d" % j, [128, 1024], F32)[:] for j in range(3)]
    psb = []
    for j in range(3):
        psb += [LP[j][:, 0:512], LP[j][:, 512:1024]]
    psb += [nc.alloc_psum_tensor("ps%d" % i, [128, 512], F32)[:] for i in range(6, 8)]

    arena = arena_h[:]
    U = U_h[:]
    cstb = cstb_h[:]
    st = st_h[:]
    fg = fg_h[:]
    ident = cstb[:, 0:128]
    nUi = cstb[:, 128:256]
    nOnes = cstb[:, 256:384]
    NEGW = cstb[:, 384:1280]

    SL = 4 * S
    QO, KO, VO = 0, 16384, 32768
    qT = arena[:, QO:QO + SL].rearrange("p (c t) -> p c t", c=4)
    kT = arena[:, KO:KO + SL].rearrange("p (c t) -> p c t", c=4)
    vv = arena[:, VO:VO + SL].rearrange("p (b f) -> p b f", f=512)
    yaT = ya_h[:].rearrange("p (c t) -> p c t", c=4)
    W3OFF = [0, 3584, 7168, 16384, 19968, 23552, 28672, 32256]
    W3c = [arena[:, o:o + 3584] for o in W3OFF]
    W3KEYS = [("arena", o, o + 3584) for o in W3OFF]
    WOUT_O, WUA_O, WUB_O = 35840, 44032, 12288
    WUA = arena[:, WUA_O:WUA_O + 4096].rearrange("p (c n) -> p c n", c=4)
    WUB = arena[:, WUB_O:WUB_O + 4096].rearrange("p (c n) -> p c n", c=4)
    WOUT = arena[:, WOUT_O:WOUT_O + 8192].rearrange("p (c n) -> p c n", c=8)

    def KA(lo, hi):
        return ("arena", lo, hi)

    def uview(off, nbytes, dt):
        assert off % 4 == 0 and nbytes % 4 == 0 and off + nbytes <= UBYTES, (off, nbytes)
        ap = U[:, off // 4:(off + nbytes) // 4]
        if dt is BF16:
            ap = ap.bitcast(BF16)
        return ap, ("U", off, off + nbytes)

    st_pos = [0]

    def st_alloc(n):
        if st_pos[0] + n > 256:
            st_pos[0] = 0
        lo = st_pos[0]
        st_pos[0] += n
        return st[:, lo:lo + n], ("st", lo, lo + n)

    rr = {"n": 0}

    def alt(*engs):
        rr["n"] += 1
        return engs[rr["n"] % len(engs)]

    c32, c32k = uview(60 * KB, 1280 * 4, F32)
    sc.add("sp", lambda e: e.dma_start(out=c32, in_=cst_d), w=[c32k], dma=True)
    sc.add("dve", lambda e: e.tensor_copy(out=cstb, in_=c32), r=[c32k], w=["cstb"])
    sc.add("sp", lambda e: e.dma_start(out=g1_h[:], in_=g1_d), w=["g1"], dma=True)
    nh = nh_h[:]
    sc.add("pool", lambda e: e.memset(nh, -0.5), w=["nh"])

    stage_i = [0]

    def load_weight_piece(src, n, dst, dk, stage_offs, scale_c=None, engs=("dve", "pool")):
        so = stage_offs[stage_i[0] % len(stage_offs)]
        stage_i[0] += 1
        if isinstance(so, tuple):
            eo = so[1]
            sv = arena[:, eo:eo + 2 * n].bitcast(F32)
            sk = KA(eo, eo + 2 * n)
        else:
            sv, sk = uview(so, n * 4, F32)
        sc.add("sp", lambda e: e.dma_start(out=sv, in_=src), w=[sk], dma=True)
        en = alt(*engs)
        if en == "act":
            if scale_c is not None:
                sc1 = g1_h[:, scale_c:scale_c + 1]
                sc.add("act", lambda e: e.activation(out=dst, in_=sv, func=AF.Copy, scale=sc1),
                       r=[sk, "g1"], w=[dk])
            else:
                sc.add("act", lambda e: e.activation(out=dst, in_=sv, func=AF.Copy), r=[sk], w=[dk])
        elif scale_c is not None:
            scl = g1_h[:, scale_c:scale_c + 1].to_broadcast([128, n])
            sc.add(en, lambda e: e.tensor_tensor(out=dst, in0=sv, in1=scl, op=ALU.mult),
                   r=[sk, "g1"], w=[dk])
        else:
            sc.add(en, lambda e: e.tensor_copy(out=dst, in_=sv), r=[sk], w=[dk])

    def rms_load(blk, xb, xbk):
        src = x_d[blk * 128:(blk + 1) * 128, :]
        sc.add("sp", lambda e: e.dma_start(out=xb, in_=src), w=[xbk], dma=True)

    def rms_head_a(xb, xbk, hb, hbk):
        ss, ssk = st_alloc(1)
        rs, rsk = st_alloc(1)
        sc.add("act", lambda e: e.activation(out=hb, in_=xb, func=AF.Square, accum_out=ss),
               r=[xbk], w=[hbk, ssk])
        sc.add("dve", lambda e: e.tensor_scalar(out=rs, in0=ss, scalar1=1.0 / D, scalar2=EPS,
                                                op0=ALU.mult, op1=ALU.add), r=[ssk], w=[rsk])
        sc.add("pool", lambda e: e.tensor_tensor(out=rs, in0=rs, in1=nh[:, 0:1], op=ALU.pow),
               r=[rsk, "nh"], w=[rsk])
        return rs, rsk

    def rms_head_b(xb, xbk, hb, hbk, rs, rsk):
        sc.add("act", lambda e: e.activation(out=hb, in_=xb, func=AF.Copy, scale=rs),
               r=[xbk, rsk], w=[hbk])

    def rms_head(blk, xb, xbk, hb, hbk, load=True):
        if load:
            rms_load(blk, xb, xbk)
        rs, rsk = rms_head_a(xb, xbk, hb, hbk)
        rms_head_b(xb, xbk, hb, hbk, rs, rsk)

    def transposes(hb, hbk, bank, dst, dstk):
        pst = psb[bank].bitcast(BF16).rearrange("p (c t) -> p c t", c=8)
        pk = PK(bank)

        def f(e):
            ins = None
            for c in range(8):
                ins = e.transpose(out=pst[:, c, :], in_=hb[:, c * 128:(c + 1) * 128], identity=ident)
            return ins
        sc.add("pe", f, r=[hbk, "cstb"], w=pk)
        en = alt("dve", "act")
        if en == "dve":
            sc.add("dve", lambda e: e.tensor_copy(out=dst, in_=pst), r=pk, w=[dstk])
        else:
            sc.add("act", lambda e: e.activation(out=dst, in_=pst, func=AF.Copy), r=pk, w=[dstk])

    def sub_key(k, i, n):
        lo, hi = k[1], k[2]
        step = (hi - lo) // n
        return (k[0], lo + i * step, lo + (i + 1) * step)

    W1f, W1k = uview(0, 32 * KB, BF16)
    W1 = W1f.rearrange("p (c n) -> p c n", c=8)
    xbs = [uview(32 * KB + 4 * KB * i, 4 * KB, F32) for i in range(2)]
    for blk_ in range(2):
        rms_load(blk_, xbs[blk_][0], xbs[blk_][1])
    hbs = [uview(40 * KB + 2 * KB * i, 2 * KB, BF16) for i in range(2)]
    hTs = [uview(44 * KB + 8 * KB * i, 8 * KB, BF16) for i in range(2)]

    mmb = [2, 3, 4, 5]
    mmi = [0]

    def next_bank():
        b = mmb[mmi[0] % len(mmb)]
        mmi[0] += 1
        return b

    def w1keys(col, n):
        return [("U", (c * 2048 + col) * 2, (c * 2048 + col + n) * 2) for c in range(8)]

    def p1_hT(t):
        hTf, hTk = hTs[t % 2]
        return hTf.rearrange("p (c t) -> p c t", c=8), hTk

    def p1_rms(t, b, load=True):
        blk = t * 4 + b
        xb, xbk = xbs[blk % 2]
        hb, hbk = hbs[blk % 2]
        rms_head(blk, xb, xbk, hb, hbk, load=load)

    def p1_T(t, b):
        blk = t * 4 + b
        hb, hbk = hbs[blk % 2]
        hT3, hTk = p1_hT(t)
        transposes(hb, hbk, blk % 2, hT3[:, :, b * 128:(b + 1) * 128], sub_key(hTk, b, 4))

    def p1_mm(t, m):
        hT3, hTk = p1_hT(t)
        t0 = t * 512
        bank = next_bank()
        ps = psb[bank]
        pk = PK(bank)
        if m < 12:
            oc = m
            col = oc * 128 if oc < 8 else 1536 + (oc - 8) * 128

            def f(e):
                ins = None
                for c in range(8):
                    ins = e.matmul(ps, lhsT=W1[:, c, col:col + 128], rhs=hT3[:, c, :],
                                   start=(c == 0), stop=(c == 7))
                return ins
            sc.add("pe", f, r=w1keys(col, 128) + [hTk], w=pk)
            if oc < 4:
                dst = qT[:, oc, t0:t0 + 512]
                dk = KA(QO + oc * S + t0, QO + oc * S + t0 + 512)
                sc.add("dve", lambda e: e.tensor_scalar(out=dst, in0=ps, scalar1=0.125, scalar2=None,
                                                        op0=ALU.mult), r=pk, w=[dk])
            elif oc < 8:
                dst = kT[:, oc - 4, t0:t0 + 512]
                dk = KA(KO + (oc - 4) * S + t0, KO + (oc - 4) * S + t0 + 512)
                sc.add("act", lambda e: e.activation(out=dst, in_=ps, func=AF.Copy), r=pk, w=[dk])
            else:
                dst = yaT[:, oc - 8, t0:t0 + 512]
                dk = ("yaT", (oc - 8) * S + t0, (oc - 8) * S + t0 + 512)
                sc.add("act", lambda e: e.activation(out=dst, in_=ps, func=AF.Silu), r=pk, w=[dk])
        else:
            b = m - 12
            blk = t * 4 + b

            def f(e):
                ins = None
                for c in range(8):
                    ins = e.matmul(ps, lhsT=hT3[:, c, b * 128:(b + 1) * 128], rhs=W1[:, c, 1024:1536],
                                   start=(c == 0), stop=(c == 7))
                return ins
            sc.add("pe", f, r=w1keys(1024, 512) + [hTk], w=pk)
            dst = vv[:, blk, :]
            dk = KA(VO + blk * 512, VO + (blk + 1) * 512)
            sc.add("dve", lambda e: e.tensor_copy(out=dst, in_=ps), r=pk, w=[dk])

    for b in range(4):
        p1_rms(0, b, load=(b >= 2))
        p1_T(0, b)
    for c0 in range(0, 2048, 1024):
        for c in range(8):
            load_weight_piece(win_d[c * 128:(c + 1) * 128, c0:c0 + 1024], 1024, W1[:, c, c0:c0 + 1024],
                              ("U", (c * 2048 + c0) * 2, (c * 2048 + c0 + 1024) * 2),
                              [64 * KB, 68 * KB] + [("arena", VO + 8192 + 2048 * j) for j in range(4)], scale_c=c)
    for t in range(NT):
        for m in range(16):
            if t + 1 < NT and m % 4 == 0:
                p1_rms(t + 1, m // 4)
            p1_mm(t, m)
            if t + 1 < NT and m % 4 == 3:
                p1_T(t + 1, m // 4)

    P3STAGE = [32 * KB, 36 * KB, 40 * KB, 44 * KB, 64 * KB, 68 * KB]

    def emit_p3_weights(part):
        p3engs = ("pool",) if part == 0 else ("pool", "dve", "act")
        def wua(c):
            load_weight_piece(wua_d[c * 128:(c + 1) * 128, :], 1024, WUA[:, c, :],
                              KA(WUA_O + c * 1024, WUA_O + (c + 1) * 1024), P3STAGE, engs=p3engs)

        def wub(c):
            load_weight_piece(wub_d[c * 128:(c + 1) * 128, :], 1024, WUB[:, c, :],
                              KA(WUB_O + c * 1024, WUB_O + (c + 1) * 1024), P3STAGE, engs=p3engs)

        def wo(c):
            load_weight_piece(wout_d[c * 128:(c + 1) * 128, :], 1024, WOUT[:, c, :],
                              KA(WOUT_O + c * 1024, WOUT_O + (c + 1) * 1024), P3STAGE, engs=p3engs)

        def w3(c):
            for c0 in range(0, 3584, 1024):
                n = min(1024, 3584 - c0)
                load_weight_piece(win_d[c * 128:(c + 1) * 128, 2048 + c0:2048 + c0 + n], n,
                                  W3c[c][:, c0:c0 + n],
                                  KA(W3OFF[c] + c0, W3OFF[c] + c0 + n), P3STAGE,
                                  scale_c=c, engs=p3engs)
        if part == 0:
            for c in range(6):
                w3(c)
        else:
            for c in (6, 7):
                w3(c)
            for c in range(4):
                wub(c)
            for c in range(4):
                wua(c)
            for c in range(8):
                wo(c)

    NE, NSP, ND, NW = 2, 3, 3, 3
    e_b = [uview(0 + 4 * KB * i, 4 * KB, F32) for i in range(NE)]
    sp_b = [uview(8 * KB + 2 * KB * i, 2 * KB, BF16) for i in range(NSP)]
    d_b = [uview(14 * KB + 2 * KB * i, 2 * KB, BF16) for i in range(ND)]
    w_b = [uview(20 * KB + 2 * KB * i, 2 * KB, BF16) for i in range(NW)]
    OBK = [6, 7]

    def h3(ap):
        return ap.rearrange("p (h t) -> p h t", h=2)
    LP3 = [h3(LP[j]) for j in range(3)]
    LPK = [PK(2 * j) + PK(2 * j + 1) for j in range(3)]

    pairs = []
    chain_id = 0
    for p in range(4):
        for qt in range(NT):
            nkb = 4 * (qt + 1)
            for i in range(nkb):
                pairs.append(dict(p=p, qt=qt, kb=nkb - 1 - i, i=i, nkb=nkb, chain=chain_id))
            chain_id += 1
    NP = len(pairs)
    for k, u in enumerate(pairs):
        u["k"] = k
        u["L"] = k % 3
        u["ob"] = OBK[u["chain"] % 2]

    def c0_of(i):
        return max(0, (3 - i) * 128)

    HP = (slice(0, 64), slice(64, 128))

    def s1(u):
        p, qt, kb = u["p"], u["qt"], u["kb"]
        c0 = c0_of(u["i"])
        diag = u["i"] <= 3
        j = u["L"]

        def f(e):
            ins = None
            for hh in range(2):
                ins = e.matmul(LP[j][:, hh * 512 + c0:(hh + 1) * 512], lhsT=kT[HP[hh], p, kb * 128:(kb + 1) * 128],
                               rhs=qT[HP[hh], p, qt * 512 + c0:(qt + 1) * 512], start=True, stop=(not diag))
            if diag:
                for hh in range(2):
                    ins = e.matmul(LP[j][:, hh * 512 + c0:hh * 512 + c0 + 128], lhsT=ident, rhs=NEGW[:, 384:512],
                                   start=False, stop=True)
            return ins
        sc.add("pe", f, r=[KA(KO + p * S + kb * 128, KO + p * S + (kb + 1) * 128),
                           KA(QO + p * S + qt * 512, QO + p * S + (qt + 1) * 512), "cstb"],
               w=LPK[j])

    def s2(u):
        c0 = c0_of(u["i"])
        L = LP3[u["L"]][:, :, c0:512]
        ev, ek = e_b[u["k"] % NE]
        evs = h3(ev)[:, :, c0:512]
        sc.add("act", lambda e: e.activation(out=evs, in_=L, func=AF.Exp), r=LPK[u["L"]], w=[ek])

    def s3(u):
        c0 = c0_of(u["i"])
        ev, ek = e_b[u["k"] % NE]
        sv, sk = sp_b[u["k"] % NSP]
        evs, svs = h3(ev)[:, :, c0:512], h3(sv)[:, :, c0:512]
        sc.add("act", lambda e: e.activation(out=svs, in_=evs, func=AF.Ln, bias=1.0), r=[ek], w=[sk])

    def dbuf(u):
        if u["i"] == 0:
            return None
        if u["i"] == 1:
            v_, k_ = sp_b[(u["k"] - 1) % NSP]
        else:
            v_, k_ = d_b[u["k"] % ND]
        return v_, k_, c0_of(u["i"] - 1)

    def s4d(u):
        if u["i"] < 2:
            return
        prev = pairs[u["k"] - 1]
        dpv, dpk, cdp = dbuf(prev)
        spv, spk = sp_b[prev["k"] % NSP]
        csp = c0_of(prev["i"])
        dv, dk = d_b[u["k"] % ND]
        d3, dp3, sp3 = h3(dv), h3(dpv), h3(spv)
        if csp < cdp:
            sc.add("dve", lambda e: e.tensor_copy(out=d3[:, :, csp:cdp], in_=sp3[:, :, csp:cdp]), r=[spk], w=[dk])
        sc.add("dve", lambda e: e.tensor_tensor(out=d3[:, :, cdp:512], in0=dp3[:, :, cdp:512], in1=sp3[:, :, cdp:512],
                                                op=ALU.add), r=[dpk, spk], w=[dk])

    def s4(u):
        c0 = c0_of(u["i"])
        j = u["L"]
        sv, sk = sp_b[u["k"] % NSP]
        sv3 = h3(sv)
        dd = dbuf(u)

        def f(e):
            ins = None
            for hh in range(2):
                ins = e.matmul(LP[j][:, hh * 512 + c0:(hh + 1) * 512], lhsT=nUi, rhs=sv3[:, hh, c0:512],
                               start=False, stop=(dd is None), skip_group_check=True)
            if dd is not None:
                cd = dd[2]
                dd3 = h3(dd[0])
                for hh in range(2):
                    ins = e.matmul(LP[j][:, hh * 512 + cd:(hh + 1) * 512], lhsT=nOnes, rhs=dd3[:, hh, cd:512],
                                   start=False, stop=True, skip_group_check=True)
            return ins
        rk = [sk, "cstb"] + ([dd[1]] if dd is not None else [])
        sc.add("pe", f, r=rk + LPK[j], w=LPK[j])

    def s5(u):
        c0 = c0_of(u["i"])
        L = LP3[u["L"]][:, :, c0:512]
        wv, wk = w_b[u["k"] % NW]
        w3 = h3(wv)
        if c0 > 0:
            sc.add("pool", lambda e: e.memset(w3[:, :, 0:c0], 0.0), w=[wk])
        sc.add("act", lambda e: e.activation(out=w3[:, :, c0:512], in_=L, func=AF.Exp), r=LPK[u["L"]], w=[wk])

    def s6(u):
        p, qt, kb = u["p"], u["qt"], u["kb"]
        O = psb[u["ob"]]
        wv, wk = w_b[u["k"] % NW]
        w3 = h3(wv)
        ok = PK(u["ob"])
        first, last = (u["i"] == 0), (u["i"] == u["nkb"] - 1)

        def f(e):
            ins = None
            for hh in range(2):
                h = 2 * p + hh
                ins = e.matmul(O[HP[hh], :], lhsT=vv[:, kb, h * 64:(h + 1) * 64], rhs=w3[:, hh, :],
                               start=first, stop=last)
            return ins
        sc.add("pe", f, r=[wk, KA(VO + kb * 512, VO + (kb + 1) * 512)] + ([] if first else ok), w=ok)
        if last:
            dst = yaT[:, p, qt * 512:(qt + 1) * 512]
            dk = ("yaT", p * S + qt * 512, p * S + (qt + 1) * 512)
            sc.add("dve", lambda e: e.tensor_tensor(out=dst, in0=O, in1=dst, op=ALU.mult),
                   r=ok + [dk], w=[dk])

    def P_(jx):
        return pairs[jx] if 0 <= jx < NP else None

    for step in range(-1, NP + 3):
        pm2, pm1, pc, pn = P_(step - 2), P_(step - 1), P_(step), P_(step + 1)
        if pm1:
            s4(pm1)
        if pc:
            s2(pc)
        if pm2:
            s5(pm2)
        if pc:
            s3(pc)
        if pn:
            s1(pn)
        if pm2:
            s6(pm2)
        if pn:
            s4d(pn)
    emit_p3_weights(0)

    lgv, lgk = uview(8 * KB, 2 * KB, F32)
    lbv, lbk = uview(10 * KB, 2 * KB, F32)
    bsv, bsk = uview(12 * KB, 2 * KB, F32)
    wsb, wsbk = uview(14 * KB, 2 * KB, BF16)
    lgT = lgv[:, 0:4]
    lbT = lgv[:, 4:8]
    negb3 = lbv.rearrange("p (c t) -> p c t", c=4)
    negbk = lbk
    sc.add("sp", lambda e: e.dma_start(out=lgv[:, 0:8], in_=lg_d), w=[lgk], dma=True)
    sc.add("sp", lambda e: e.dma_start(out=bsv, in_=bsT_d), w=[bsk], dma=True)
    sc.add("sp", lambda e: e.dma_start(out=fg, in_=fg_d), w=["fg"], dma=True)
    ws32, ws32k = uview(0, 4 * KB, F32)
    mk32, mk32k = uview(4 * KB, 4 * KB, F32)
    sc.add("sp", lambda e: e.dma_start(out=ws32, in_=wsT_d), w=[ws32k], dma=True)
    sc.add("sp", lambda e: e.dma_start(out=mk32, in_=msk_d), w=[mk32k], dma=True)
    sc.add("pool", lambda e: e.tensor_tensor(out=wsb, in0=ws32, in1=mk32, op=ALU.mult),
           r=[ws32k, mk32k], w=[wsbk])
    wsT3 = wsb.rearrange("p (g t) -> p g t", g=8)
    bs3 = bsv.rearrange("p (c t) -> p c t", c=4)

    def f_rowsum(e):
        ins = None
        for g in range(8):
            fc_, gg = g // 2, g % 2
            ins = e.matmul(psb[3][gg * 64:(gg + 1) * 64, fc_ * 128:(fc_ + 1) * 128],
                           lhsT=nOnes[:, 0:64], rhs=wsT3[:, g, :], start=True, stop=True)
        return ins
    sc.add("pe", f_rowsum, r=[wsbk, "cstb"], w=PK(3))
    for fc_ in range(4):
        sc.add("dve", lambda e, fc_=fc_: e.scalar_tensor_tensor(
            out=negb3[:, fc_, :], in0=psb[3][:, fc_ * 128:(fc_ + 1) * 128], scalar=lbT[:, fc_:fc_ + 1],
            in1=bs3[:, fc_, :], op0=ALU.mult, op1=ALU.subtract),
            r=PK(3) + [lgk, bsk], w=[negbk])

    xb3 = [uview(16 * KB + 4 * KB * i, 4 * KB, F32) for i in range(2)]
    hb3 = [uview(24 * KB + 2 * KB * i, 2 * KB, BF16) for i in range(2)]
    hT3f, hT3k = uview(28 * KB, 4 * KB, BF16)
    hTt = hT3f.rearrange("p (c t) -> p c t", c=8)
    tmpA = [uview(32 * KB + 2 * KB * i, 2 * KB, F32) for i in range(2)]
    gvb = [uview(36 * KB + 2 * KB * i, 2 * KB, F32) for i in range(2)]
    vnf, vnk = uview(40 * KB, 2 * KB, BF16)
    vn3 = vnf.rearrange("p (b f) -> p b f", b=2)
    szb = [uview(42 * KB + 512 * i, 512, BF16) for i in range(2)]
    gsf, gsk = uview(44 * KB, 2 * KB, BF16)
    gs3 = gsf.rearrange("p (c t) -> p c t", c=4)
    ybf, ybk = uview(46 * KB, 2 * KB, BF16)
    yb3 = ybf.rearrange("p (c t) -> p c t", c=4)
    taf, tak = uview(48 * KB, 4 * KB, BF16)
    ta3 = taf.rearrange("p (c t) -> p c t", c=8)
    tbf, tbk = uview(52 * KB, 4 * KB, BF16)
    tb3 = tbf.rearrange("p (c t) -> p c t", c=8)
    m1b = [uview(56 * KB + KB * i, KB, F32) for i in range(2)]
    m2b = [uview(58 * KB + KB * i, KB, F32) for i in range(2)]
    mTf, mTk = uview(60 * KB, 4 * KB, BF16)
    mT3 = mTf.rearrange("p (c t) -> p c t", c=8)
    xrb = [uview(64 * KB + 4 * KB * i, 4 * KB, F32) for i in range(2)]

    HB_B = (4, 5, 6, 7, 1, 2)
    HB_CD = (4, 5, 6, 7)
    hb_state = {"lst": HB_B, "i": 0}

    def set_rotation(lst):
        hb_state["lst"], hb_state["i"] = lst, 0

    def next_half():
        lst = hb_state["lst"]
        bk = lst[hb_state["i"] % len(lst)]
        hb_state["i"] += 1
        return psb[bk][:, 0:256], PK(bk)

    def gelu_p1(ps, pk, tA, tAk, out, outk):
        sc.add("act", lambda e: e.activation(out=out, in_=ps, func=AF.Copy), r=pk, w=[outk])
        sc.add("act", lambda e: e.activation(out=tA, in_=ps, func=AF.Square), r=pk, w=[tAk])

    def gelu_p2(tA, tAk, out, outk):
        sc.add("dve", lambda e: e.tensor_scalar(out=tA, in0=tA, scalar1=C_G1 * C_G2, scalar2=C_G1,
                                                op0=ALU.mult, op1=ALU.add), r=[tAk], w=[tAk])
        sc.add("dve", lambda e: e.tensor_tensor(out=tA, in0=tA, in1=out, op=ALU.mult), r=[tAk, outk], w=[tAk])
        sc.add("act", lambda e: e.activation(out=tA, in_=tA, func=AF.Tanh), r=[tAk], w=[tAk])
        sc.add("dve", lambda e: e.scalar_tensor_tensor(out=out, in0=tA, scalar=1.0, in1=out,
                                                       op0=ALU.add, op1=ALU.mult), r=[tAk, outk], w=[outk])

    out_dmas = []
    _order = list(range(NT3))

    def p3_head_dma(t):
        for b in range(2):
            blk = t * 2 + b
            xb, xbk = xb3[blk % 2]
            rms_load(blk, xb, xbk)

    def p3_head_rms(t):
        for b in range(2):
            blk = t * 2 + b
            xb, xbk = xb3[blk % 2]
            hb, hbk = hb3[blk % 2]
            rms_head(blk, xb, xbk, hb, hbk, load=False)

    def p3_head_rms_a(t):
        out = []
        for b in range(2):
            blk = t * 2 + b
            xb, xbk = xb3[blk % 2]
            hb, hbk = hb3[blk % 2]
            out.append(rms_head_a(xb, xbk, hb, hbk))
        return out

    def p3_head_rms_b(t, rss):
        for b in range(2):
            blk = t * 2 + b
            xb, xbk = xb3[blk % 2]
            hb, hbk = hb3[blk % 2]
            rms_head_b(xb, xbk, hb, hbk, rss[b][0], rss[b][1])

    def p3_head_T(t):
        for b in range(2):
            blk = t * 2 + b
            hb, hbk = hb3[blk % 2]
            transposes(hb, hbk, 0, hTt[:, :, b * 128:(b + 1) * 128], sub_key(hT3k, b, 2))

    GATES = [(w_, oc_) for w_ in (0, 1) for oc_ in range(8)]

    def emit_gates(lst):
        for which, oc in lst:
            t3, tk, coff = ((ta3, tak, 1536), (tb3, tbk, 2560))[which]
            ps, pk = next_half()

            def fg_(e, oc=oc, ps=ps, coff=coff):
                ins = None
                for c in range(8):
                    ins = e.matmul(ps, lhsT=W3c[c][:, coff + oc * 128:coff + (oc + 1) * 128], rhs=hTt[:, c, :],
                                   start=(c == 0), stop=(c == 7))
                return ins
            sc.add("pe", fg_, r=[hT3k] + W3KEYS, w=pk)
            dst = t3[:, oc, :]
            sc.add("act", lambda e, dst=dst, ps=ps: e.activation(out=dst, in_=ps, func=AF.Tanh, scale=0.5),
                   r=pk, w=[sub_key(tk, oc, 8)])

    def p3_B(t, pre_gates_hook=None, mid_gates_hook=None, after_vb_hook=None):
        tok0 = t * 256
        set_rotation(HB_B)
        vchains = []
        for b in range(2):
            bank = 1 + b
            ps = psb[bank]
            pk = PK(bank)

            def f(e, b=b, ps=ps):
                ins = None
                for c in range(8):
                    ins = e.matmul(ps, lhsT=hTt[:, c, b * 128:(b + 1) * 128], rhs=W3c[c][:, 512:1024],
                                   start=(c == 0), stop=(c == 7))
                return ins
            sc.add("pe", f, r=[hT3k] + W3KEYS, w=pk)
            tA, tAk = tmpA[b]
            gv, gvk = gvb[b]
            gelu_p1(ps, pk, tA, tAk, gv, gvk)
            sc.capture()
            gelu_p2(tA, tAk, gv, gvk)
            vchains.append(sc.end_capture())
        if after_vb_hook is not None:
            after_vb_hook()

        def ln_chain(b):
            tA, tAk = tmpA[b]
            gv, gvk = gvb[b]
            gv3 = gv.rearrange("p (g c) -> p g c", g=8)
            tA3 = tA.rearrange("p (g c) -> p g c", g=8)
            s1_, s1k = st_alloc(8)
            s2_, s2k = st_alloc(8)
            qq, qqk = st_alloc(8)
            vr, vrk = st_alloc(8)
            s1b = s1_.unsqueeze(2).to_broadcast([128, 8, 64])
            vrb = vr.unsqueeze(2).to_broadcast([128, 8, 64])
            vdst = vn3[:, b, :].rearrange("p (g c) -> p g c", g=8)
            sc.add("dve", lambda e: e.tensor_reduce(out=s1_, in_=gv3, axis=AX.X, op=ALU.add), r=[gvk], w=[s1k])
            sc.add("act", lambda e: e.activation(out=tA, in_=gv, func=AF.Square), r=[gvk], w=[tAk])
            sc.add("dve", lambda e: e.tensor_reduce(out=s2_, in_=tA3, axis=AX.X, op=ALU.add), r=[tAk], w=[s2k])
            sc.add("dve", lambda e: e.scalar_tensor_tensor(out=gv3, in0=gv3, scalar=64.0, in1=s1b,
                                                           op0=ALU.mult, op1=ALU.subtract), r=[gvk, s1k], w=[gvk])
            sc.add("dve", lambda e: e.tensor_tensor(out=qq, in0=s1_, in1=s1_, op=ALU.mult), r=[s1k], w=[qqk])
            sc.add("dve", lambda e: e.scalar_tensor_tensor(out=vr, in0=s2_, scalar=64.0, in1=qq,
                                                           op0=ALU.mult, op1=ALU.subtract), r=[s2k, qqk], w=[vrk])
            sc.add("dve", lambda e: e.tensor_scalar(out=vr, in0=vr, scalar1=4096.0 * 4.0 * EPS, scalar2=None,
                                                    op0=ALU.add), r=[vrk], w=[vrk])
            sc.add("pool", lambda e: e.tensor_tensor(out=vr, in0=vr, in1=nh, op=ALU.pow), r=[vrk, "nh"], w=[vrk])
            sc.add("dve", lambda e: e.tensor_tensor(out=vdst, in0=gv3, in1=vrb, op=ALU.mult),
                   r=[gvk, vrk], w=[sub_key(vnk, b, 2)])
        for fc0 in (0, 2):
            uch = []
            pend = []
            for fc in (fc0, fc0 + 1):
                psu, pku = next_half()
                psz, pkz = next_half()

                def fu(e, fc=fc, psu=psu):
                    ins = None
                    for c in range(8):
                        ins = e.matmul(psu, lhsT=W3c[c][:, fc * 128:(fc + 1) * 128], rhs=hTt[:, c, :],
                                       start=(c == 0), stop=(c == 7))
                    return ins

                def fz(e, fc=fc, psz=psz):
                    ins = None
                    for c in range(8):
                        ins = e.matmul(psz, lhsT=W3c[c][:, 1024 + fc * 128:1024 + (fc + 1) * 128], rhs=hTt[:, c, :],
                                       start=(c == 0), stop=(c == 7))
                    return ins
                sc.add("pe", fu, r=[hT3k] + W3KEYS, w=pku)
                sc.add("pe", fz, r=[hT3k] + W3KEYS, w=pkz)
                pend.append((fc, psu, pku, psz, pkz))
            for fc, psu, pku, psz, pkz in pend:
                tA, tAk = m2b[fc % 2]
                gu, guk = m1b[fc % 2]
                sz, szk = szb[fc % 2]
                gdst = gs3[:, fc, :]
                gelu_p1(psu, pku, tA, tAk, gu, guk)
                sc.add("act", lambda e, sz=sz, psz=psz: e.activation(out=sz, in_=psz, func=AF.Silu), r=pkz, w=[szk])
                sc.capture()
                gelu_p2(tA, tAk, gu, guk)
                sc.add("pool", lambda e, gdst=gdst, gu=gu, sz=sz: e.tensor_tensor(out=gdst, in0=gu, in1=sz, op=ALU.mult),
                       r=[guk, szk], w=[sub_key(gsk, fc, 4)])
                uch.append(sc.end_capture())
            emit_gates(GATES[0:2] if fc0 == 0 else GATES[2:4])
            if fc0 == 0:
                sc.interleave(vchains + uch)
            else:
                sc.interleave(uch)
                lch = []
                for b in range(2):
                    sc.capture()
                    ln_chain(b)
                    lch.append(sc.end_capture())
                sc.interleave(lch)
        if pre_gates_hook is not None:
            pre_gates_hook()
        emit_gates(GATES[4:8])
        if mid_gates_hook is not None:
            mid_gates_hook()
        emit_gates(GATES[8:14])

    def p3_CD(t, mid_hook=None):
        tok0 = t * 256
        set_rotation(HB_CD)
        for fc in range(4):
            ps, pk = next_half()

            def fs(e, fc=fc, ps=ps):
                ins = None
                for b in range(2):
                    for gg in range(2):
                        g = 2 * fc + gg
                        ins = e.matmul(ps[gg * 64:(gg + 1) * 64, b * 128:(b + 1) * 128],
                                       lhsT=vn3[:, b, g * 64:(g + 1) * 64], rhs=wsT3[:, g, :],
                                       start=True, stop=True)
                return ins
            sc.add("pe", fs, r=[vnk, wsbk], w=pk)
            m1, m1k = (m1b + m2b)[fc]
            nbb = negb3[:, fc, :].unsqueeze(1).to_broadcast([128, 2, 128])
            ps3 = ps.rearrange("p (b t) -> p b t", b=2)
            m13 = m1.rearrange("p (b t) -> p b t", b=2)
            lgc = lgT[:, fc:fc + 1]
            sc.add("dve", lambda e, m13=m13, ps3=ps3, nbb=nbb, lgc=lgc: e.scalar_tensor_tensor(
                out=m13, in0=ps3, scalar=lgc, in1=nbb, op0=ALU.mult, op1=ALU.subtract),
                r=pk + [negbk, lgk], w=[m1k])
        for fc in range(4):
            m1, m1k = (m1b + m2b)[fc]
            ydst = yb3[:, fc, :]
            gsrc = gs3[:, fc, :]
            sc.add("dve", lambda e, ydst=ydst, m1=m1, gsrc=gsrc: e.scalar_tensor_tensor(
                out=ydst, in0=m1, scalar=0.5, in1=gsrc, op0=ALU.mult, op1=ALU.mult),
                r=[m1k, sub_key(gsk, fc, 4)], w=[sub_key(ybk, fc, 4)])
        emit_gates(GATES[14:16])
        if mid_hook is not None:
            mid_hook()
        for oc in range(8):
            psa, pka = next_half()
            psb_, pkb = next_half()

            def fa(e, oc=oc, psa=psa, tok0=tok0):
                ins = None
                for c in range(4):
                    ins = e.matmul(psa, lhsT=WUA[:, c, oc * 128:(oc + 1) * 128], rhs=yaT[:, c, tok0:tok0 + 256],
                                   start=(c == 0), stop=(c == 3))
                return ins

            def fb(e, oc=oc, psb_=psb_):
                ins = None
                for c in range(4):
                    ins = e.matmul(psb_, lhsT=WUB[:, c, oc * 128:(oc + 1) * 128], rhs=yb3[:, c, :],
                                   start=(c == 0), stop=(c == 3))
                return ins
            sc.add("pe", fa, r=[KA(WUA_O, WUA_O + 4096)] + [("yaT", c * S + tok0, c * S + tok0 + 256) for c in range(4)], w=pka)
            if oc == 0:
                for c in range(4):
                    sc.add("pe", lambda e, c=c, psb_=psb_: e.matmul(
                        psb_, lhsT=WUB[:, c, 0:128], rhs=yb3[:, c, :], start=(c == 0), stop=(c == 3)),
                        r=[KA(WUB_O, WUB_O + 4096), sub_key(ybk, c, 4)], w=pkb)
            else:
                sc.add("pe", fb, r=[KA(WUB_O, WUB_O + 4096), ybk], w=pkb)
            m1, m1k = m1b[oc % 2]
            m2, m2k = m2b[oc % 2]
            tas = ta3[:, oc, :]
            tbs = tb3[:, oc, :]
            sc.add("dve", lambda e, m1=m1, tas=tas, psa=psa: e.scalar_tensor_tensor(
                out=m1, in0=tas, scalar=1.0, in1=psa, op0=ALU.add, op1=ALU.mult),
                r=pka + [sub_key(tak, oc, 8)], w=[m1k])
            sc.add("dve", lambda e, m2=m2, tbs=tbs, psb_=psb_: e.scalar_tensor_tensor(
                out=m2, in0=tbs, scalar=1.0, in1=psb_, op0=ALU.add, op1=ALU.mult),
                r=pkb + [sub_key(tbk, oc, 8)], w=[m2k])
            mdst = mT3[:, oc, :]
            sc.add("pool", lambda e, mdst=mdst, m1=m1, m2=m2: e.tensor_tensor(out=mdst, in0=m1, in1=m2, op=ALU.add),
                   r=[m1k, m2k], w=[sub_key(mTk, oc, 8)])
    def p3_E(t):
        tok0 = t * 256
        for b in range(2):
            blk = t * 2 + b
            xr, xrk = xrb[blk % 2]
            src = x_d[blk * 128:(blk + 1) * 128, :]
            sc.add("sp", lambda e, xr=xr, src=src: e.dma_start(out=xr, in_=src), w=[xrk], dma=True)
            for half in range(2):
                bank = (3, 0)[half]
                ps = psb[bank]
                pk = PK(bank)

                def fo(e, b=b, half=half, ps=ps):
                    ins = None
                    for c in range(8):
                        ins = e.matmul(ps, lhsT=mT3[:, c, b * 128:(b + 1) * 128],
                                       rhs=WOUT[:, c, half * 512:(half + 1) * 512],
                                       start=(c == 0), stop=(c == 7))
                    return ins
                sc.add("pe", fo, r=[mTk, KA(WOUT_O, WOUT_O + 8192)], w=pk)
                xh = xr[:, half * 512:(half + 1) * 512]
                sc.add("dve", lambda e, xh=xh, ps=ps: e.scalar_tensor_tensor(
                    out=xh, in0=ps, scalar=0.5, in1=xh, op0=ALU.mult, op1=ALU.add),
                    r=pk + [xrk], w=[xrk])

    def p3_E_fin(t):
        for b in range(2):
            blk = t * 2 + b
            xr, xrk = xrb[blk % 2]
            ss, ssk = st_alloc(1)
            rs, rsk = st_alloc(1)
            junk = mTf[:, 0:1024]
            sc.add("act", lambda e, junk=junk, xr=xr, ss=ss: e.activation(out=junk, in_=xr, func=AF.Square, accum_out=ss),
                   r=[xrk], w=[mTk, ssk])
            sc.add("dve", lambda e, rs=rs, ss=ss: e.tensor_scalar(out=rs, in0=ss, scalar1=1.0 / D, scalar2=EPS,
                                                                  op0=ALU.mult, op1=ALU.add), r=[ssk], w=[rsk])
            sc.add("pool", lambda e, rs=rs: e.tensor_tensor(out=rs, in0=rs, in1=nh[:, 0:1], op=ALU.pow),
                   r=[rsk, "nh"], w=[rsk])
            sc.add("dve", lambda e, xr=xr, rs=rs: e.scalar_tensor_tensor(
                out=xr, in0=xr, scalar=rs, in1=fg, op0=ALU.mult, op1=ALU.mult),
                r=[xrk, rsk, "fg"], w=[xrk])
            dst = out_d[blk * 128:(blk + 1) * 128, :]
            od = sc.add("sp", lambda e, xr=xr, dst=dst: e.dma_start(out=dst, in_=xr), r=[xrk], w=[], dma=True)
            out_dmas.append(od)

    p3_head_dma(_order[0])
    p3_head_rms(_order[0])
    p3_head_T(_order[0])
    emit_p3_weights(1)
    for _ti, t in enumerate(_order):
        nxt = _order[_ti + 1] if _ti + 1 < len(_order) else None
        _rss = []

        def _next_head_a(nxt=nxt, _rss=_rss):
            p3_head_dma(nxt)
            _rss.extend(p3_head_rms_a(nxt))

        def _next_head_b(nxt=nxt, _rss=_rss):
            p3_head_rms_b(nxt, _rss)
        prev = _order[_ti - 1] if _ti > 0 else None
        p3_B(t, pre_gates_hook=_next_head_a if nxt is not None else None,
             mid_gates_hook=_next_head_b if nxt is not None else None,
             after_vb_hook=(lambda prev=prev: p3_E_fin(prev)) if prev is not None else None)
        p3_CD(t, mid_hook=(lambda nxt=nxt: p3_head_T(nxt)) if nxt is not None else None)
        p3_E(t)
        if nxt is None:
            p3_E_fin(t)

    sc.add("sp", lambda e: None, extra=out_dmas)

    sc.finalize()
    with nc.Block() as block:
        @block.sync
        def _(eng):
            sc.emit("sp", eng)

        @block.tensor
        def _(eng):
            sc.emit("pe", eng)

        @block.scalar
        def _(eng):
            sc.emit("act", eng)

        @block.vector
        def _(eng):
            sc.emit("dve", eng)

        @block.gpsimd
        def _(eng):
            sc.emit("pool", eng)
    return nc


def _consts():
    c = np.zeros((128, 1280), np.float32)
    j = np.arange(128)[:, None]
    s = np.arange(128)[None, :]
    c[:, 0:128] = np.eye(128, dtype=np.float32)
    c[:, 128:256] = np.where(j >= s, -1.0, 0.0)
    c[:, 256:384] = -1.0
    jj = np.arange(896)[None, :]
    c[:, 384:1280] = np.where((jj - 384) <= j, NEG, 0.0)
    m = np.zeros((128, 8, 128), np.float32)
    sblk = (np.arange(128) // 64)[:, None]
    tblk = (np.arange(128) // 64)[None, :]
    m[:] = np.where(sblk <= tblk, 1.0, 0.0)[:, None, :]
    return c, m.reshape(128, 1024)


_NC_CACHE = {}


def kernel(x, norm_g, w_in, sgu_ln_g, sgu_ln_b, w_spatial, b_spatial, w_up_a, w_up_b, w_out, final_norm_g):
    x = np.asarray(x, np.float32)
    B, S, _ = x.shape
    f32 = lambda a: np.ascontiguousarray(np.asarray(a, np.float32))
    cst, msk = _consts()
    g1 = f32(np.asarray(norm_g)[0].reshape(8, 128).T)
    fg = f32(np.broadcast_to(np.asarray(final_norm_g).reshape(1, D), (128, D)))
    lg = f32(np.concatenate([np.asarray(sgu_ln_g)[0].reshape(4, 128).T,
                             np.asarray(sgu_ln_b)[0].reshape(4, 128).T], axis=1))
    bs = np.asarray(b_spatial)[0]
    bsT = np.empty((128, 4, 128), np.float32)
    for g in range(8):
        bsT[(g % 2) * 64:(g % 2) * 64 + 64, g // 2, :] = bs[g][None, :]
    wsT = f32(np.transpose(np.asarray(w_spatial)[0], (2, 0, 1)).reshape(128, 1024))
    common = {
        "w_in": f32(np.asarray(w_in)[0]), "w_up_a": f32(np.asarray(w_up_a)[0]),
        "w_up_b": f32(np.asarray(w_up_b)[0]), "w_out": f32(np.asarray(w_out)[0]),
        "g1": g1, "fg": fg, "lg": lg, "bsT": f32(bsT.reshape(128, 512)),
        "wsT": wsT, "msk": f32(msk), "cst": f32(cst),
    }
    if S not in _NC_CACHE:
        _NC_CACHE[S] = build_nc(S)
    nc = _NC_CACHE[S]
    in_maps = []
    for b in range(B):
        m = dict(common)
        m["x"] = f32(x[b])
        in_maps.append(m)
    res = run_bass_kernel_spmd(nc, in_maps, core_ids=list(range(B)))
    return np.stack([np.asarray(r["out"], np.float32).reshape(S, D) for r in res.results], axis=0)
```

```python
import numpy as np
import concourse.bass as bass
import concourse.mybir as mybir
from concourse.bass_utils import run_bass_kernel_spmd

F32 = mybir.dt.float32
BF16 = mybir.dt.bfloat16
AF = mybir.ActivationFunctionType
ALU = mybir.AluOpType
AX = mybir.AxisListType

D = 1024
DIN = 5632
EPS = 1e-6
NEG = -30000.0
C_G1 = 0.7978845608028654
C_G2 = 0.044715
N_DMA_SEMS = 24
INF = 1 << 60
KB = 1024


class Op:
    __slots__ = ("eng", "fn", "deps", "dma", "signal", "val", "sem", "prev_val", "waits", "idx")


class Sched:
    ENGS = ("pe", "act", "dve", "pool", "sp")

    def __init__(self, nc):
        self.nc = nc
        self.q = {e: [] for e in self.ENGS}
        self.all = []
        self.acc = {}
        self.esem = {}
        self.dsems = []
        self._cap = None

    @staticmethod
    def _norm(k):
        if isinstance(k, tuple):
            return k[0], k[1], k[2]
        return k, 0, INF

    def capture(self):
        self._cap = []

    def end_capture(self):
        c, self._cap = self._cap, None
        return c

    def interleave(self, chains):
        idx = [0] * len(chains)
        left = sum(len(c) for c in chains)
        while left:
            for i, c in enumerate(chains):
                if idx[i] < len(c):
                    self.add(*c[idx[i]])
                    idx[i] += 1
                    left -= 1

    def add(self, eng, fn, r=(), w=(), extra=(), dma=False):
        if self._cap is not None:
            self._cap.append((eng, fn, tuple(r), tuple(w), tuple(extra), dma))
            return None
        op = Op()
        op.eng, op.fn, op.dma = eng, fn, dma
        op.signal, op.val, op.sem, op.prev_val, op.waits = False, 0, None, 0, []
        op.idx = len(self.all)
        deps = set(e for e in extra if e is not None)
        rn = [self._norm(k) for k in r]
        wn = [self._norm(k) for k in w]
        for name, lo, hi in rn:
            for rec in self.acc.get(name, ()):
                if rec[3] and rec[0] < hi and lo < rec[1]:
                    deps.add(rec[2])
        for name, lo, hi in wn:
            for rec in self.acc.get(name, ()):
                if rec[0] < hi and lo < rec[1]:
                    deps.add(rec[2])
        for name, lo, hi in rn:
            self.acc.setdefault(name, []).append([lo, hi, op, False])
        for name, lo, hi in wn:
            lst = self.acc.setdefault(name, [])
            lst[:] = [rec for rec in lst if not (lo <= rec[0] and rec[1] <= hi)]
            lst.append([lo, hi, op, True])
        deps.discard(op)
        op.deps = deps
        self.q[eng].append(op)
        self.all.append(op)
        return op

    def finalize(self):
        nc = self.nc
        for e in self.ENGS:
            self.esem[e] = nc.alloc_semaphore("sem_" + e)
        self.dsems = [nc.alloc_semaphore("dsem%d" % i) for i in range(N_DMA_SEMS)]
        for op in self.all:
            for d in op.deps:
                if d.dma:
                    continue
                if d.eng == op.eng and d.eng == "pe":
                    continue
                d.signal = True
        uses = [0] * N_DMA_SEMS
        ndma = 0
        for e in self.ENGS:
            cnt = 0
            for op in self.q[e]:
                if op.dma:
                    i = ndma % N_DMA_SEMS
                    ndma += 1
                    op.sem = self.dsems[i]
                    op.prev_val = 16 * uses[i]
                    uses[i] += 1
                    op.val = 16 * uses[i]
                elif op.signal:
                    cnt += 1
                    op.val = cnt
        for e in self.ENGS:
            waited = {}
            for op in self.q[e]:
                need = {}
                for d in op.deps:
                    if d.dma:
                        s, v = d.sem, d.val
                    else:
                        if d.eng == e and e == "pe":
                            continue
                        s, v = self.esem[d.eng], d.val
                    if need.get(s.num, (None, 0))[1] < v:
                        need[s.num] = (s, v)
                if op.dma and op.prev_val > 0:
                    if need.get(op.sem.num, (None, 0))[1] < op.prev_val:
                        need[op.sem.num] = (op.sem, op.prev_val)
                op.waits = []
                for num, (s, v) in need.items():
                    if waited.get(num, 0) < v:
                        op.waits.append((s, v))
                        waited[num] = v

    def emit(self, e, eng):
        for op in self.q[e]:
            for (s, v) in op.waits:
                eng.wait_ge(s, v)
            ins = op.fn(eng)
            if op.dma:
                ins.then_inc(op.sem, 16)
            elif op.signal:
                ins.then_inc(self.esem[e], 1)


def PK(bank, ph=None, c0=0, c1=512):
    name = "ps%d" % bank
    halves = (0, 1) if ph is None else (ph,)
    return [(name, h * 1024 + c0, h * 1024 + c1) for h in halves]


def build_nc(S):
    NB = S // 128
    NT = S // 512
    NT3 = S // 256
    nc = bass.Bass("TRN2", target_bir_lowering=False)

    def din(name, shape):
        return nc.dram_tensor(name, list(shape), F32, kind="ExternalInput").ap()

    x_d = din("x", (S, D))
    win_d = din("w_in", (D, DIN))
    wua_d = din("w_up_a", (512, D))
    wub_d = din("w_up_b", (512, D))
    wout_d = din("w_out", (D, D))
    g1_d = din("g1", (128, 8))
    fg_d = din("fg", (128, D))
    lg_d = din("lg", (128, 8))
    bsT_d = din("bsT", (128, 512))
    wsT_d = din("wsT", (128, 1024))
    msk_d = din("msk", (128, 1024))
    cst_d = din("cst", (128, 1280))
    out_d = nc.dram_tensor("out", [S, D], F32, kind="ExternalOutput").ap()

    sc = Sched(nc)

    ARENA_E = 49152
    UBYTES = 72 * KB
    arena_h = nc.alloc_sbuf_tensor("arena", [128, ARENA_E], BF16)
    ya_h = nc.alloc_sbuf_tensor("yaT", [128, 4 * S], BF16)
    cstb_h = nc.alloc_sbuf_tensor("cstb", [128, 1280], BF16)
    g1_h = nc.alloc_sbuf_tensor("g1s", [128, 8], F32)
    fg_h = nc.alloc_sbuf_tensor("fgs", [128, D], F32)
    st_h = nc.alloc_sbuf_tensor("st", [128, 256], F32)
    nh_h = nc.alloc_sbuf_tensor("nh", [128, 8], F32)
    U_h = nc.alloc_sbuf_tensor("U", [128, UBYTES // 4], F32)
    LP = [nc.alloc_psum_tensor("lp%d" % j, [128, 1024], F32)[:] for j in range(3)]
    psb = []
    for j in range(3):
        psb += [LP[j][:, 0:512], LP[j][:, 512:1024]]
    psb += [nc.alloc_psum_tensor("ps%d" % i, [128, 512], F32)[:] for i in range(6, 8)]

    arena = arena_h[:]
    U = U_h[:]
    cstb = cstb_h[:]
    st = st_h[:]
    fg = fg_h[:]
    ident = cstb[:, 0:128]
    nUi = cstb[:, 128:256]
    nOnes = cstb[:, 256:384]
    NEGW = cstb[:, 384:1280]

    SL = 4 * S
    QO, KO, VO = 0, 16384, 32768
    qT = arena[:, QO:QO + SL].rearrange("p (c t) -> p c t", c=4)
    kT = arena[:, KO:KO + SL].rearrange("p (c t) -> p c t", c=4)
    vv = arena[:, VO:VO + SL].rearrange("p (b f) -> p b f", f=512)
    yaT = ya_h[:].rearrange("p (c t) -> p c t", c=4)
    W3OFF = [0, 3584, 7168, 16384, 19968, 23552, 28672, 32256]
    W3c = [arena[:, o:o + 3584] for o in W3OFF]
    W3KEYS = [("arena", o, o + 3584) for o in W3OFF]
    WOUT_O, WUA_O, WUB_O = 35840, 44032, 12288
    WUA = arena[:, WUA_O:WUA_O + 4096].rearrange("p (c n) -> p c n", c=4)
    WUB = arena[:, WUB_O:WUB_O + 4096].rearrange("p (c n) -> p c n", c=4)
    WOUT = arena[:, WOUT_O:WOUT_O + 8192].rearrange("p (c n) -> p c n", c=8)

    def KA(lo, hi):
        return ("arena", lo, hi)

    def uview(off, nbytes, dt):
        assert off % 4 == 0 and nbytes % 4 == 0 and off + nbytes <= UBYTES, (off, nbytes)
        ap = U[:, off // 4:(off + nbytes) // 4]
        if dt is BF16:
            ap = ap.bitcast(BF16)
        return ap, ("U", off, off + nbytes)

    st_pos = [0]

    def st_alloc(n):
        if st_pos[0] + n > 256:
            st_pos[0] = 0
        lo = st_pos[0]
        st_pos[0] += n
        return st[:, lo:lo + n], ("st", lo, lo + n)

    rr = {"n": 0}

    def alt(*engs):
        rr["n"] += 1
        return engs[rr["n"] % len(engs)]

    c32, c32k = uview(60 * KB, 1280 * 4, F32)
    sc.add("sp", lambda e: e.dma_start(out=c32, in_=cst_d), w=[c32k], dma=True)
    sc.add("dve", lambda e: e.tensor_copy(out=cstb, in_=c32), r=[c32k], w=["cstb"])
    sc.add("sp", lambda e: e.dma_start(out=g1_h[:], in_=g1_d), w=["g1"], dma=True)
    nh = nh_h[:]
    sc.add("pool", lambda e: e.memset(nh, -0.5), w=["nh"])
    sc.add("sp", lambda e: e.dma_start(out=fg, in_=fg_d), w=["fg"], dma=True)

    stage_i = [0]

    def load_weight_piece(src, n, dst, dk, stage_offs, scale_c=None, engs=("dve", "pool")):
        so = stage_offs[stage_i[0] % len(stage_offs)]
        stage_i[0] += 1
        if isinstance(so, tuple):
            eo = so[1]
            sv = arena[:, eo:eo + 2 * n].bitcast(F32)
            sk = KA(eo, eo + 2 * n)
        else:
            sv, sk = uview(so, n * 4, F32)
        sc.add("sp", lambda e: e.dma_start(out=sv, in_=src), w=[sk], dma=True)
        en = alt(*engs)
        if en == "act":
            if scale_c is not None:
                sc1 = g1_h[:, scale_c:scale_c + 1]
                sc.add("act", lambda e: e.activation(out=dst, in_=sv, func=AF.Copy, scale=sc1),
                       r=[sk, "g1"], w=[dk])
            else:
                sc.add("act", lambda e: e.activation(out=dst, in_=sv, func=AF.Copy), r=[sk], w=[dk])
        elif scale_c is not None:
            scl = g1_h[:, scale_c:scale_c + 1].to_broadcast([128, n])
            sc.add(en, lambda e: e.tensor_tensor(out=dst, in0=sv, in1=scl, op=ALU.mult),
                   r=[sk, "g1"], w=[dk])
        else:
            sc.add(en, lambda e: e.tensor_copy(out=dst, in_=sv), r=[sk], w=[dk])

    def rms_load(blk, xb, xbk):
        src = x_d[blk * 128:(blk + 1) * 128, :]
        sc.add("sp", lambda e: e.dma_start(out=xb, in_=src), w=[xbk], dma=True)

    def rms_head_a(xb, xbk, hb, hbk):
        ss, ssk = st_alloc(1)
        rs, rsk = st_alloc(1)
        sc.add("act", lambda e: e.activation(out=hb, in_=xb, func=AF.Square, accum_out=ss),
               r=[xbk], w=[hbk, ssk])
        sc.add("dve", lambda e: e.tensor_scalar(out=rs, in0=ss, scalar1=1.0 / D, scalar2=EPS,
                                                op0=ALU.mult, op1=ALU.add), r=[ssk], w=[rsk])
        sc.add("pool", lambda e: e.tensor_tensor(out=rs, in0=rs, in1=nh[:, 0:1], op=ALU.pow),
               r=[rsk, "nh"], w=[rsk])
        return rs, rsk

    def rms_head_b(xb, xbk, hb, hbk, rs, rsk):
        sc.add("act", lambda e: e.activation(out=hb, in_=xb, func=AF.Copy, scale=rs),
               r=[xbk, rsk], w=[hbk])

    def rms_head(blk, xb, xbk, hb, hbk, load=True):
        if load:
            rms_load(blk, xb, xbk)
        rs, rsk = rms_head_a(xb, xbk, hb, hbk)
        rms_head_b(xb, xbk, hb, hbk, rs, rsk)

    def transposes(hb, hbk, bank, dst, dstk):
        pst = psb[bank].bitcast(BF16).rearrange("p (c t) -> p c t", c=8)
        pk = PK(bank)

        def f(e):
            ins = None
            for c in range(8):
                ins = e.transpose(out=pst[:, c, :], in_=hb[:, c * 128:(c + 1) * 128], identity=ident)
            return ins
        sc.add("pe", f, r=[hbk, "cstb"], w=pk)
        en = alt("dve", "act")
        if en == "dve":
            sc.add("dve", lambda e: e.tensor_copy(out=dst, in_=pst), r=pk, w=[dstk])
        else:
            sc.add("act", lambda e: e.activation(out=dst, in_=pst, func=AF.Copy), r=pk, w=[dstk])

    def sub_key(k, i, n):
        lo, hi = k[1], k[2]
        step = (hi - lo) // n
        return (k[0], lo + i * step, lo + (i + 1) * step)

    W1f, W1k = uview(0, 32 * KB, BF16)
    W1 = W1f.rearrange("p (c n) -> p c n", c=8)
    xbs = [uview(32 * KB + 4 * KB * i, 4 * KB, F32) for i in range(2)]
    for blk_ in range(2):
        rms_load(blk_, xbs[blk_][0], xbs[blk_][1])
    hbs = [uview(40 * KB + 2 * KB * i, 2 * KB, BF16) for i in range(2)]
    hTs = [uview(44 * KB + 8 * KB * i, 8 * KB, BF16) for i in range(2)]

    mmb = [2, 3, 4, 5]
    mmi = [0]

    def next_bank():
        b = mmb[mmi[0] % len(mmb)]
        mmi[0] += 1
        return b

    def w1keys(col, n):
        return [("U", (c * 2048 + col) * 2, (c * 2048 + col + n) * 2) for c in range(8)]

    def p1_hT(t):
        hTf, hTk = hTs[t % 2]
        return hTf.rearrange("p (c t) -> p c t", c=8), hTk

    def p1_rms(t, b, load=True):
        blk = t * 4 + b
        xb, xbk = xbs[blk % 2]
        hb, hbk = hbs[blk % 2]
        rms_head(blk, xb, xbk, hb, hbk, load=load)

    def p1_T(t, b):
        blk = t * 4 + b
        hb, hbk = hbs[blk % 2]
        hT3, hTk = p1_hT(t)
        transposes(hb, hbk, blk % 2, hT3[:, :, b * 128:(b + 1) * 128], sub_key(hTk, b, 4))

    def p1_mm(t, m):
        hT3, hTk = p1_hT(t)
        t0 = t * 512
        bank = next_bank()
        ps = psb[bank]
        pk = PK(bank)
        if m < 12:
            oc = m
            col = oc * 128 if oc < 8 else 1536 + (oc - 8) * 128

            def f(e):
                ins = None
                for c in range(8):
                    ins = e.matmul(ps, lhsT=W1[:, c, col:col + 128], rhs=hT3[:, c, :],
                                   start=(c == 0), stop=(c == 7))
                return ins
            sc.add("pe", f, r=w1keys(col, 128) + [hTk], w=pk)
            if oc < 4:
                dst = qT[:, oc, t0:t0 + 512]
                dk = KA(QO + oc * S + t0, QO + oc * S + t0 + 512)
                sc.add("dve", lambda e: e.tensor_scalar(out=dst, in0=ps, scalar1=0.125, scalar2=None,
                                                        op0=ALU.mult), r=pk, w=[dk])
            elif oc < 8:
                dst = kT[:, oc - 4, t0:t0 + 512]
                dk = KA(KO + (oc - 4) * S + t0, KO + (oc - 4) * S + t0 + 512)
                sc.add("act", lambda e: e.activation(out=dst, in_=ps, func=AF.Copy), r=pk, w=[dk])
            else:
                dst = yaT[:, oc - 8, t0:t0 + 512]
                dk = ("yaT", (oc - 8) * S + t0, (oc - 8) * S + t0 + 512)
                sc.add("act", lambda e: e.activation(out=dst, in_=ps, func=AF.Silu), r=pk, w=[dk])
        else:
            b = m - 12
            blk = t * 4 + b

            def f(e):
                ins = None
                for c in range(8):
                    ins = e.matmul(ps, lhsT=hT3[:, c, b * 128:(b + 1) * 128], rhs=W1[:, c, 1024:1536],
                                   start=(c == 0), stop=(c == 7))
                return ins
            sc.add("pe", f, r=w1keys(1024, 512) + [hTk], w=pk)
            dst = vv[:, blk, :]
            dk = KA(VO + blk * 512, VO + (blk + 1) * 512)
            sc.add("dve", lambda e: e.tensor_copy(out=dst, in_=ps), r=pk, w=[dk])

    for b in range(4):
        p1_rms(0, b, load=(b >= 2))
        p1_T(0, b)
    for c0 in range(0, 2048, 1024):
        for c in range(8):
            load_weight_piece(win_d[c * 128:(c + 1) * 128, c0:c0 + 1024], 1024, W1[:, c, c0:c0 + 1024],
                              ("U", (c * 2048 + c0) * 2, (c * 2048 + c0 + 1024) * 2),
                              [64 * KB, 68 * KB] + [("arena", VO + 8192 + 2048 * j) for j in range(4)], scale_c=c)
    for t in range(NT):
        for m in range(16):
            if t + 1 < NT and m % 4 == 0:
                p1_rms(t + 1, m // 4)
            p1_mm(t, m)
            if t + 1 < NT and m % 4 == 3:
                p1_T(t + 1, m // 4)

    P3STAGE = [32 * KB, 36 * KB, 40 * KB, 44 * KB, 64 * KB, 68 * KB]

    def emit_p3_weights(part):
        p3engs = ("pool",) if part == 0 else ("pool", "dve", "act")
        def wua(c):
            load_weight_piece(wua_d[c * 128:(c + 1) * 128, :], 1024, WUA[:, c, :],
                              KA(WUA_O + c * 1024, WUA_O + (c + 1) * 1024), P3STAGE, engs=p3engs)

        def wub(c):
            load_weight_piece(wub_d[c * 128:(c + 1) * 128, :], 1024, WUB[:, c, :],
                              KA(WUB_O + c * 1024, WUB_O + (c + 1) * 1024), P3STAGE, engs=p3engs)

        def wo(c):
            load_weight_piece(wout_d[c * 128:(c + 1) * 128, :], 1024, WOUT[:, c, :],
                              KA(WOUT_O + c * 1024, WOUT_O + (c + 1) * 1024), P3STAGE, engs=p3engs)

        def w3(c):
            for c0 in range(0, 3584, 1024):
                n = min(1024, 3584 - c0)
                load_weight_piece(win_d[c * 128:(c + 1) * 128, 2048 + c0:2048 + c0 + n], n,
                                  W3c[c][:, c0:c0 + n],
                                  KA(W3OFF[c] + c0, W3OFF[c] + c0 + n), P3STAGE,
                                  scale_c=c, engs=p3engs)
        if part == 0:
            for c in range(6):
                w3(c)
        else:
            for c in (6, 7):
                w3(c)
            for c in range(4):
                wub(c)
            for c in range(4):
                wua(c)
            for c in range(8):
                wo(c)

    NE, NSP, ND, NW = 2, 3, 3, 3
    e_b = [uview(0 + 4 * KB * i, 4 * KB, F32) for i in range(NE)]
    sp_b = [uview(8 * KB + 2 * KB * i, 2 * KB, BF16) for i in range(NSP)]
    d_b = [uview(14 * KB + 2 * KB * i, 2 * KB, BF16) for i in range(ND)]
    w_b = [uview(20 * KB + 2 * KB * i, 2 * KB, BF16) for i in range(NW)]
    OBK = [6, 7]

    def h3(ap):
        return ap.rearrange("p (h t) -> p h t", h=2)
    LP3 = [h3(LP[j]) for j in range(3)]
    LPK = [PK(2 * j) + PK(2 * j + 1) for j in range(3)]

    pairs = []
    chain_id = 0
    for p in range(4):
        for qt in range(NT):
            nkb = 4 * (qt + 1)
            for i in range(nkb):
                pairs.append(dict(p=p, qt=qt, kb=nkb - 1 - i, i=i, nkb=nkb, chain=chain_id))
            chain_id += 1
    NP = len(pairs)
    for k, u in enumerate(pairs):
        u["k"] = k
        u["L"] = k % 3
        u["ob"] = OBK[u["chain"] % 2]

    def c0_of(i):
        return max(0, (3 - i) * 128)

    HP = (slice(0, 64), slice(64, 128))

    def s1(u):
        p, qt, kb = u["p"], u["qt"], u["kb"]
        c0 = c0_of(u["i"])
        diag = u["i"] <= 3
        j = u["L"]

        def f(e):
            ins = None
            for hh in range(2):
                ins = e.matmul(LP[j][:, hh * 512 + c0:(hh + 1) * 512], lhsT=kT[HP[hh], p, kb * 128:(kb + 1) * 128],
                               rhs=qT[HP[hh], p, qt * 512 + c0:(qt + 1) * 512], start=True, stop=(not diag))
            if diag:
                for hh in range(2):
                    ins = e.matmul(LP[j][:, hh * 512 + c0:hh * 512 + c0 + 128], lhsT=ident, rhs=NEGW[:, 384:512],
                                   start=False, stop=True)
            return ins
        sc.add("pe", f, r=[KA(KO + p * S + kb * 128, KO + p * S + (kb + 1) * 128),
                           KA(QO + p * S + qt * 512, QO + p * S + (qt + 1) * 512), "cstb"],
               w=LPK[j])

    def s2(u):
        c0 = c0_of(u["i"])
        L = LP3[u["L"]][:, :, c0:512]
        ev, ek = e_b[u["k"] % NE]
        evs = h3(ev)[:, :, c0:512]
        sc.add("act", lambda e: e.activation(out=evs, in_=L, func=AF.Exp), r=LPK[u["L"]], w=[ek])

    def s3(u):
        c0 = c0_of(u["i"])
        ev, ek = e_b[u["k"] % NE]
        sv, sk = sp_b[u["k"] % NSP]
        evs, svs = h3(ev)[:, :, c0:512], h3(sv)[:, :, c0:512]
        sc.add("act", lambda e: e.activation(out=svs, in_=evs, func=AF.Ln, bias=1.0), r=[ek], w=[sk])

    def dbuf(u):
        if u["i"] == 0:
            return None
        if u["i"] == 1:
            v_, k_ = sp_b[(u["k"] - 1) % NSP]
        else:
            v_, k_ = d_b[u["k"] % ND]
        return v_, k_, c0_of(u["i"] - 1)

    def s4d(u):
        if u["i"] < 2:
            return
        prev = pairs[u["k"] - 1]
        dpv, dpk, cdp = dbuf(prev)
        spv, spk = sp_b[prev["k"] % NSP]
        csp = c0_of(prev["i"])
        dv, dk = d_b[u["k"] % ND]
        d3, dp3, sp3 = h3(dv), h3(dpv), h3(spv)
        if csp < cdp:
            sc.add("dve", lambda e: e.tensor_copy(out=d3[:, :, csp:cdp], in_=sp3[:, :, csp:cdp]), r=[spk], w=[dk])
        sc.add("dve", lambda e: e.tensor_tensor(out=d3[:, :, cdp:512], in0=dp3[:, :, cdp:512], in1=sp3[:, :, cdp:512],
                                                op=ALU.add), r=[dpk, spk], w=[dk])

    def s4(u):
        c0 = c0_of(u["i"])
        j = u["L"]
        sv, sk = sp_b[u["k"] % NSP]
        sv3 = h3(sv)
        dd = dbuf(u)

        def f(e):
            ins = None
            for hh in range(2):
                ins = e.matmul(LP[j][:, hh * 512 + c0:(hh + 1) * 512], lhsT=nUi, rhs=sv3[:, hh, c0:512],
                               start=False, stop=(dd is None), skip_group_check=True)
            if dd is not None:
                cd = dd[2]
                dd3 = h3(dd[0])
                for hh in range(2):
                    ins = e.matmul(LP[j][:, hh * 512 + cd:(hh + 1) * 512], lhsT=nOnes, rhs=dd3[:, hh, cd:512],
                                   start=False, stop=True, skip_group_check=True)
            return ins
        rk = [sk, "cstb"] + ([dd[1]] if dd is not None else [])
        sc.add("pe", f, r=rk + LPK[j], w=LPK[j])

    def s5(u):
        c0 = c0_of(u["i"])
        L = LP3[u["L"]][:, :, c0:512]
        wv, wk = w_b[u["k"] % NW]
        w3 = h3(wv)
        if c0 > 0:
            sc.add("pool", lambda e: e.memset(w3[:, :, 0:c0], 0.0), w=[wk])
        sc.add("act", lambda e: e.activation(out=w3[:, :, c0:512], in_=L, func=AF.Exp), r=LPK[u["L"]], w=[wk])

    def s6(u):
        p, qt, kb = u["p"], u["qt"], u["kb"]
        O = psb[u["ob"]]
        wv, wk = w_b[u["k"] % NW]
        w3 = h3(wv)
        ok = PK(u["ob"])
        first, last = (u["i"] == 0), (u["i"] == u["nkb"] - 1)

        def f(e):
            ins = None
            for hh in range(2):
                h = 2 * p + hh
                ins = e.matmul(O[HP[hh], :], lhsT=vv[:, kb, h * 64:(h + 1) * 64], rhs=w3[:, hh, :],
                               start=first, stop=last)
            return ins
        sc.add("pe", f, r=[wk, KA(VO + kb * 512, VO + (kb + 1) * 512)] + ([] if first else ok), w=ok)
        if last:
            dst = yaT[:, p, qt * 512:(qt + 1) * 512]
            dk = ("yaT", p * S + qt * 512, p * S + (qt + 1) * 512)
            sc.add("dve", lambda e: e.tensor_tensor(out=dst, in0=O, in1=dst, op=ALU.mult),
                   r=ok + [dk], w=[dk])

    def P_(jx):
        return pairs[jx] if 0 <= jx < NP else None

    for step in range(-1, NP + 3):
        pm2, pm1, pc, pn = P_(step - 2), P_(step - 1), P_(step), P_(step + 1)
        if pm1:
            s4(pm1)
        if pc:
            s2(pc)
        if pm2:
            s5(pm2)
        if pc:
            s3(pc)
        if pn:
            s1(pn)
        if pm2:
            s6(pm2)
        if pn:
            s4d(pn)
    emit_p3_weights(0)

    lgv, lgk = uview(8 * KB, 2 * KB, F32)
    lbv, lbk = uview(10 * KB, 2 * KB, F32)
    bsv, bsk = uview(12 * KB, 2 * KB, F32)
    wsb, wsbk = uview(14 * KB, 2 * KB, BF16)
    lgT = lgv[:, 0:4]
    lbT = lgv[:, 4:8]
    negb3 = lbv.rearrange("p (c t) -> p c t", c=4)
    negbk = lbk
    sc.add("sp", lambda e: e.dma_start(out=lgv[:, 0:8], in_=lg_d), w=[lgk], dma=True)
    sc.add("sp", lambda e: e.dma_start(out=bsv, in_=bsT_d), w=[bsk], dma=True)
    ws32, ws32k = uview(0, 4 * KB, F32)
    mk32, mk32k = uview(4 * KB, 4 * KB, F32)
    sc.add("sp", lambda e: e.dma_start(out=ws32, in_=wsT_d), w=[ws32k], dma=True)
    sc.add("sp", lambda e: e.dma_start(out=mk32, in_=msk_d), w=[mk32k], dma=True)
    sc.add("pool", lambda e: e.tensor_tensor(out=wsb, in0=ws32, in1=mk32, op=ALU.mult),
           r=[ws32k, mk32k], w=[wsbk])
    wsT3 = wsb.rearrange("p (g t) -> p g t", g=8)
    bs3 = bsv.rearrange("p (c t) -> p c t", c=4)

    def f_rowsum(e):
        ins = None
        for g in range(8):
            fc_, gg = g // 2, g % 2
            ins = e.matmul(psb[3][gg * 64:(gg + 1) * 64, fc_ * 128:(fc_ + 1) * 128],
                           lhsT=nOnes[:, 0:64], rhs=wsT3[:, g, :], start=True, stop=True)
        return ins
    sc.add("pe", f_rowsum, r=[wsbk, "cstb"], w=PK(3))
    for fc_ in range(4):
        sc.add("dve", lambda e, fc_=fc_: e.scalar_tensor_tensor(
            out=negb3[:, fc_, :], in0=psb[3][:, fc_ * 128:(fc_ + 1) * 128], scalar=lbT[:, fc_:fc_ + 1],
            in1=bs3[:, fc_, :], op0=ALU.mult, op1=ALU.subtract),
            r=PK(3) + [lgk, bsk], w=[negbk])

    xb3 = [uview(16 * KB + 4 * KB * i, 4 * KB, F32) for i in range(2)]
    hb3 = [uview(24 * KB + 2 * KB * i, 2 * KB, BF16) for i in range(2)]
    hT3f, hT3k = uview(28 * KB, 4 * KB, BF16)
    hTt = hT3f.rearrange("p (c t) -> p c t", c=8)
    tmpA = [uview(32 * KB + 2 * KB * i, 2 * KB, F32) for i in range(2)]
    gvb = [uview(36 * KB + 2 * KB * i, 2 * KB, F32) for i in range(2)]
    vnf, vnk = uview(40 * KB, 2 * KB, BF16)
    vn3 = vnf.rearrange("p (b f) -> p b f", b=2)
    szb = [uview(42 * KB + 512 * i, 512, BF16) for i in range(2)]
    gsf, gsk = uview(44 * KB, 2 * KB, BF16)
    gs3 = gsf.rearrange("p (c t) -> p c t", c=4)
    ybf, ybk = uview(46 * KB, 2 * KB, BF16)
    yb3 = ybf.rearrange("p (c t) -> p c t", c=4)
    taf, tak = uview(48 * KB, 4 * KB, BF16)
    ta3 = taf.rearrange("p (c t) -> p c t", c=8)
    tbf, tbk = uview(52 * KB, 4 * KB, BF16)
    tb3 = tbf.rearrange("p (c t) -> p c t", c=8)
    m1b = [uview(56 * KB + KB * i, KB, F32) for i in range(2)]
    m2b = [uview(58 * KB + KB * i, KB, F32) for i in range(2)]
    mTf, mTk = uview(60 * KB, 4 * KB, BF16)
    mT3 = mTf.rearrange("p (c t) -> p c t", c=8)
    xrb = [uview(64 * KB + 4 * KB * i, 4 * KB, F32) for i in range(2)]

    HB_B = (4, 5, 6, 7, 1, 2)
    HB_CD = (4, 5, 6, 7)
    hb_state = {"lst": HB_B, "i": 0}

    def set_rotation(lst):
        hb_state["lst"], hb_state["i"] = lst, 0

    def next_half():
        lst = hb_state["lst"]
        bk = lst[hb_state["i"] % len(lst)]
        hb_state["i"] += 1
        return psb[bk][:, 0:256], PK(bk)

    def gelu_p1(ps, pk, tA, tAk, out, outk):
        sc.add("act", lambda e: e.activation(out=out, in_=ps, func=AF.Copy), r=pk, w=[outk])
        sc.add("act", lambda e: e.activation(out=tA, in_=ps, func=AF.Square), r=pk, w=[tAk])

    def gelu_p2(tA, tAk, out, outk):
        sc.add("dve", lambda e: e.tensor_scalar(out=tA, in0=tA, scalar1=C_G1 * C_G2, scalar2=C_G1,
                                                op0=ALU.mult, op1=ALU.add), r=[tAk], w=[tAk])
        sc.add("dve", lambda e: e.tensor_tensor(out=tA, in0=tA, in1=out, op=ALU.mult), r=[tAk, outk], w=[tAk])
        sc.add("act", lambda e: e.activation(out=tA, in_=tA, func=AF.Tanh), r=[tAk], w=[tAk])
        sc.add("dve", lambda e: e.scalar_tensor_tensor(out=out, in0=tA, scalar=1.0, in1=out,
                                                       op0=ALU.add, op1=ALU.mult), r=[tAk, outk], w=[outk])

    out_dmas = []
    _order = list(range(NT3))

    def p3_head_dma(t):
        for b in range(2):
            blk = t * 2 + b
            xb, xbk = xb3[blk % 2]
            rms_load(blk, xb, xbk)

    def p3_head_rms(t):
        for b in range(2):
            blk = t * 2 + b
            xb, xbk = xb3[blk % 2]
            hb, hbk = hb3[blk % 2]
            rms_head(blk, xb, xbk, hb, hbk, load=False)

    def p3_head_rms_a(t):
        out = []
        for b in range(2):
            blk = t * 2 + b
            xb, xbk = xb3[blk % 2]
            hb, hbk = hb3[blk % 2]
            out.append(rms_head_a(xb, xbk, hb, hbk))
        return out

    def p3_head_rms_b(t, rss):
        for b in range(2):
            blk = t * 2 + b
            xb, xbk = xb3[blk % 2]
            hb, hbk = hb3[blk % 2]
            rms_head_b(xb, xbk, hb, hbk, rss[b][0], rss[b][1])

    def p3_head_T(t):
        for b in range(2):
            blk = t * 2 + b
            hb, hbk = hb3[blk % 2]
            transposes(hb, hbk, 0, hTt[:, :, b * 128:(b + 1) * 128], sub_key(hT3k, b, 2))

    GATES = [(w_, oc_) for w_ in (0, 1) for oc_ in range(8)]

    def emit_gates(lst):
        for which, oc in lst:
            t3, tk, coff = ((ta3, tak, 1536), (tb3, tbk, 2560))[which]
            ps, pk = next_half()

            def fg_(e, oc=oc, ps=ps, coff=coff):
                ins = None
                for c in range(8):
                    ins = e.matmul(ps, lhsT=W3c[c][:, coff + oc * 128:coff + (oc + 1) * 128], rhs=hTt[:, c, :],
                                   start=(c == 0), stop=(c == 7))
                return ins
            sc.add("pe", fg_, r=[hT3k] + W3KEYS, w=pk)
            dst = t3[:, oc, :]
            sc.add("act", lambda e, dst=dst, ps=ps: e.activation(out=dst, in_=ps, func=AF.Tanh, scale=0.5),
                   r=pk, w=[sub_key(tk, oc, 8)])

    def p3_B(t, pre_gates_hook=None, mid_gates_hook=None, after_vb_hook=None):
        tok0 = t * 256
        set_rotation(HB_B)
        vchains = []
        for b in range(2):
            bank = 1 + b
            ps = psb[bank]
            pk = PK(bank)

            def f(e, b=b, ps=ps):
                ins = None
                for c in range(8):
                    ins = e.matmul(ps, lhsT=hTt[:, c, b * 128:(b + 1) * 128], rhs=W3c[c][:, 512:1024],
                                   start=(c == 0), stop=(c == 7))
                return ins
            sc.add("pe", f, r=[hT3k] + W3KEYS, w=pk)
            tA, tAk = tmpA[b]
            gv, gvk = gvb[b]
            gelu_p1(ps, pk, tA, tAk, gv, gvk)
            sc.capture()
            gelu_p2(tA, tAk, gv, gvk)
            vchains.append(sc.end_capture())
        if after_vb_hook is not None:
            after_vb_hook()

        def ln_chain(b):
            tA, tAk = tmpA[b]
            gv, gvk = gvb[b]
            gv3 = gv.rearrange("p (g c) -> p g c", g=8)
            tA3 = tA.rearrange("p (g c) -> p g c", g=8)
            s1_, s1k = st_alloc(8)
            s2_, s2k = st_alloc(8)
            qq, qqk = st_alloc(8)
            vr, vrk = st_alloc(8)
            s1b = s1_.unsqueeze(2).to_broadcast([128, 8, 64])
            vrb = vr.unsqueeze(2).to_broadcast([128, 8, 64])
            vdst = vn3[:, b, :].rearrange("p (g c) -> p g c", g=8)
            sc.add("dve", lambda e: e.tensor_reduce(out=s1_, in_=gv3, axis=AX.X, op=ALU.add), r=[gvk], w=[s1k])
            sc.add("act", lambda e: e.activation(out=tA, in_=gv, func=AF.Square), r=[gvk], w=[tAk])
            sc.add("dve", lambda e: e.tensor_reduce(out=s2_, in_=tA3, axis=AX.X, op=ALU.add), r=[tAk], w=[s2k])
            sc.add("dve", lambda e: e.scalar_tensor_tensor(out=gv3, in0=gv3, scalar=64.0, in1=s1b,
                                                           op0=ALU.mult, op1=ALU.subtract), r=[gvk, s1k], w=[gvk])
            sc.add("dve", lambda e: e.tensor_tensor(out=qq, in0=s1_, in1=s1_, op=ALU.mult), r=[s1k], w=[qqk])
            sc.add("dve", lambda e: e.scalar_tensor_tensor(out=vr, in0=s2_, scalar=64.0, in1=qq,
                                                           op0=ALU.mult, op1=ALU.subtract), r=[s2k, qqk], w=[vrk])
            sc.add("dve", lambda e: e.tensor_scalar(out=vr, in0=vr, scalar1=4096.0 * 4.0 * EPS, scalar2=None,
                                                    op0=ALU.add), r=[vrk], w=[vrk])
            sc.add("pool", lambda e: e.tensor_tensor(out=vr, in0=vr, in1=nh, op=ALU.pow), r=[vrk, "nh"], w=[vrk])
            sc.add("dve", lambda e: e.tensor_tensor(out=vdst, in0=gv3, in1=vrb, op=ALU.mult),
                   r=[gvk, vrk], w=[sub_key(vnk, b, 2)])
        for fc0 in (0, 2):
            uch = []
            pend = []
            for fc in (fc0, fc0 + 1):
                psu, pku = next_half()
                psz, pkz = next_half()

                def fu(e, fc=fc, psu=psu):
                    ins = None
                    for c in range(8):
                        ins = e.matmul(psu, lhsT=W3c[c][:, fc * 128:(fc + 1) * 128], rhs=hTt[:, c, :],
                                       start=(c == 0), stop=(c == 7))
                    return ins

                def fz(e, fc=fc, psz=psz):
                    ins = None
                    for c in range(8):
                        ins = e.matmul(psz, lhsT=W3c[c][:, 1024 + fc * 128:1024 + (fc + 1) * 128], rhs=hTt[:, c, :],
                                       start=(c == 0), stop=(c == 7))
                    return ins
                sc.add("pe", fu, r=[hT3k] + W3KEYS, w=pku)
                sc.add("pe", fz, r=[hT3k] + W3KEYS, w=pkz)
                pend.append((fc, psu, pku, psz, pkz))
            for fc, psu, pku, psz, pkz in pend:
                tA, tAk = m2b[fc % 2]
                gu, guk = m1b[fc % 2]
                sz, szk = szb[fc % 2]
                gdst = gs3[:, fc, :]
                gelu_p1(psu, pku, tA, tAk, gu, guk)
                sc.add("act", lambda e, sz=sz, psz=psz: e.activation(out=sz, in_=psz, func=AF.Silu), r=pkz, w=[szk])
                sc.capture()
                gelu_p2(tA, tAk, gu, guk)
                sc.add("pool", lambda e, gdst=gdst, gu=gu, sz=sz: e.tensor_tensor(out=gdst, in0=gu, in1=sz, op=ALU.mult),
                       r=[guk, szk], w=[sub_key(gsk, fc, 4)])
                uch.append(sc.end_capture())
            emit_gates(GATES[0:2] if fc0 == 0 else GATES[2:4])
            if fc0 == 0:
                sc.interleave(vchains + uch)
            else:
                sc.interleave(uch)
                lch = []
                for b in range(2):
                    sc.capture()
                    ln_chain(b)
                    lch.append(sc.end_capture())
                sc.interleave(lch)
        if pre_gates_hook is not None:
            pre_gates_hook()
        emit_gates(GATES[4:8])
        if mid_gates_hook is not None:
            mid_gates_hook()
        emit_gates(GATES[8:14])

    def p3_CD(t, mid_hook=None):
        tok0 = t * 256
        set_rotation(HB_CD)
        for fc in range(4):
            ps, pk = next_half()

            def fs(e, fc=fc, ps=ps):
                ins = None
                for b in range(2):
                    for gg in range(2):
                        g = 2 * fc + gg
                        ins = e.matmul(ps[gg * 64:(gg + 1) * 64, b * 128:(b + 1) * 128],
                                       lhsT=vn3[:, b, g * 64:(g + 1) * 64], rhs=wsT3[:, g, :],
                                       start=True, stop=True)
                return ins
            sc.add("pe", fs, r=[vnk, wsbk], w=pk)
            m1, m1k = (m1b + m2b)[fc]
            nbb = negb3[:, fc, :].unsqueeze(1).to_broadcast([128, 2, 128])
            ps3 = ps.rearrange("p (b t) -> p b t", b=2)
            m13 = m1.rearrange("p (b t) -> p b t", b=2)
            lgc = lgT[:, fc:fc + 1]
            sc.add("dve", lambda e, m13=m13, ps3=ps3, nbb=nbb, lgc=lgc: e.scalar_tensor_tensor(
                out=m13, in0=ps3, scalar=lgc, in1=nbb, op0=ALU.mult, op1=ALU.subtract),
                r=pk + [negbk, lgk], w=[m1k])
        for fc in range(4):
            m1, m1k = (m1b + m2b)[fc]
            ydst = yb3[:, fc, :]
            gsrc = gs3[:, fc, :]
            sc.add("dve", lambda e, ydst=ydst, m1=m1, gsrc=gsrc: e.scalar_tensor_tensor(
                out=ydst, in0=m1, scalar=0.5, in1=gsrc, op0=ALU.mult, op1=ALU.mult),
                r=[m1k, sub_key(gsk, fc, 4)], w=[sub_key(ybk, fc, 4)])
        emit_gates(GATES[14:16])
        if mid_hook is not None:
            mid_hook()
        for oc in range(8):
            psa, pka = next_half()
            psb_, pkb = next_half()

            def fa(e, oc=oc, psa=psa, tok0=tok0):
                ins = None
                for c in range(4):
                    ins = e.matmul(psa, lhsT=WUA[:, c, oc * 128:(oc + 1) * 128], rhs=yaT[:, c, tok0:tok0 + 256],
                                   start=(c == 0), stop=(c == 3))
                return ins

            def fb(e, oc=oc, psb_=psb_):
                ins = None
                for c in range(4):
                    ins = e.matmul(psb_, lhsT=WUB[:, c, oc * 128:(oc + 1) * 128], rhs=yb3[:, c, :],
                                   start=(c == 0), stop=(c == 3))
                return ins
            sc.add("pe", fa, r=[KA(WUA_O, WUA_O + 4096)] + [("yaT", c * S + tok0, c * S + tok0 + 256) for c in range(4)], w=pka)
            if oc == 0:
                for c in range(4):
                    sc.add("pe", lambda e, c=c, psb_=psb_: e.matmul(
                        psb_, lhsT=WUB[:, c, 0:128], rhs=yb3[:, c, :], start=(c == 0), stop=(c == 3)),
                        r=[KA(WUB_O, WUB_O + 4096), sub_key(ybk, c, 4)], w=pkb)
            else:
                sc.add("pe", fb, r=[KA(WUB_O, WUB_O + 4096), ybk], w=pkb)
            m1, m1k = m1b[oc % 2]
            m2, m2k = m2b[oc % 2]
            tas = ta3[:, oc, :]
            tbs = tb3[:, oc, :]
            sc.add("dve", lambda e, m1=m1, tas=tas, psa=psa: e.scalar_tensor_tensor(
                out=m1, in0=tas, scalar=1.0, in1=psa, op0=ALU.add, op1=ALU.mult),
                r=pka + [sub_key(tak, oc, 8)], w=[m1k])
            sc.add("dve", lambda e, m2=m2, tbs=tbs, psb_=psb_: e.scalar_tensor_tensor(
                out=m2, in0=tbs, scalar=1.0, in1=psb_, op0=ALU.add, op1=ALU.mult),
                r=pkb + [sub_key(tbk, oc, 8)], w=[m2k])
            mdst = mT3[:, oc, :]
            sc.add("pool", lambda e, mdst=mdst, m1=m1, m2=m2: e.tensor_tensor(out=mdst, in0=m1, in1=m2, op=ALU.add),
                   r=[m1k, m2k], w=[sub_key(mTk, oc, 8)])
    def p3_E(t):
        tok0 = t * 256
        for b in range(2):
            blk = t * 2 + b
            xr, xrk = xrb[blk % 2]
            src = x_d[blk * 128:(blk + 1) * 128, :]
            sc.add("sp", lambda e, xr=xr, src=src: e.dma_start(out=xr, in_=src), w=[xrk], dma=True)
            for half in range(2):
                bank = (3, 0)[half]
                ps = psb[bank]
                pk = PK(bank)

                def fo(e, b=b, half=half, ps=ps):
                    ins = None
                    for c in range(8):
                        ins = e.matmul(ps, lhsT=mT3[:, c, b * 128:(b + 1) * 128],
                                       rhs=WOUT[:, c, half * 512:(half + 1) * 512],
                                       start=(c == 0), stop=(c == 7))
                    return ins
                sc.add("pe", fo, r=[mTk, KA(WOUT_O, WOUT_O + 8192)], w=pk)
                xh = xr[:, half * 512:(half + 1) * 512]
                sc.add("dve", lambda e, xh=xh, ps=ps: e.scalar_tensor_tensor(
                    out=xh, in0=ps, scalar=0.5, in1=xh, op0=ALU.mult, op1=ALU.add),
                    r=pk + [xrk], w=[xrk])

    def p3_E_fin(t):
        for b in range(2):
            blk = t * 2 + b
            xr, xrk = xrb[blk % 2]
            ss, ssk = st_alloc(1)
            rs, rsk = st_alloc(1)
            junk = mTf[:, 0:1024]
            sc.add("act", lambda e, junk=junk, xr=xr, ss=ss: e.activation(out=junk, in_=xr, func=AF.Square, accum_out=ss),
                   r=[xrk], w=[mTk, ssk])
            sc.add("dve", lambda e, rs=rs, ss=ss: e.tensor_scalar(out=rs, in0=ss, scalar1=1.0 / D, scalar2=EPS,
                                                                  op0=ALU.mult, op1=ALU.add), r=[ssk], w=[rsk])
            sc.add("pool", lambda e, rs=rs: e.tensor_tensor(out=rs, in0=rs, in1=nh[:, 0:1], op=ALU.pow),
                   r=[rsk, "nh"], w=[rsk])
            sc.add("dve", lambda e, xr=xr, rs=rs: e.scalar_tensor_tensor(
                out=xr, in0=xr, scalar=rs, in1=fg, op0=ALU.mult, op1=ALU.mult),
                r=[xrk, rsk, "fg"], w=[xrk])
            dst = out_d[blk * 128:(blk + 1) * 128, :]
            od = sc.add("sp", lambda e, xr=xr, dst=dst: e.dma_start(out=dst, in_=xr), r=[xrk], w=[], dma=True)
            out_dmas.append(od)

    p3_head_dma(_order[0])
    p3_head_rms(_order[0])
    p3_head_T(_order[0])
    emit_p3_weights(1)
    for _ti, t in enumerate(_order):
        nxt = _order[_ti + 1] if _ti + 1 < len(_order) else None
        _rss = []

        def _next_head_a(nxt=nxt, _rss=_rss):
            p3_head_dma(nxt)
            _rss.extend(p3_head_rms_a(nxt))

        def _next_head_b(nxt=nxt, _rss=_rss):
            p3_head_rms_b(nxt, _rss)
        prev = _order[_ti - 1] if _ti > 0 else None
        p3_B(t, pre_gates_hook=_next_head_a if nxt is not None else None,
             mid_gates_hook=_next_head_b if nxt is not None else None,
             after_vb_hook=(lambda prev=prev: p3_E_fin(prev)) if prev is not None else None)
        p3_CD(t, mid_hook=(lambda nxt=nxt: p3_head_T(nxt)) if nxt is not None else None)
        p3_E(t)
        if nxt is None:
            p3_E_fin(t)

    sc.add("sp", lambda e: None, extra=out_dmas)

    sc.finalize()
    with nc.Block() as block:
        @block.sync
        def _(eng):
            sc.emit("sp", eng)

        @block.tensor
        def _(eng):
            sc.emit("pe", eng)

        @block.scalar
        def _(eng):
            sc.emit("act", eng)

        @block.vector
        def _(eng):
            sc.emit("dve", eng)

        @block.gpsimd
        def _(eng):
            sc.emit("pool", eng)
    return nc


def _consts():
    c = np.zeros((128, 1280), np.float32)
    j = np.arange(128)[:, None]
    s = np.arange(128)[None, :]
    c[:, 0:128] = np.eye(128, dtype=np.float32)
    c[:, 128:256] = np.where(j >= s, -1.0, 0.0)
    c[:, 256:384] = -1.0
    jj = np.arange(896)[None, :]
    c[:, 384:1280] = np.where((jj - 384) <= j, NEG, 0.0)
    m = np.zeros((128, 8, 128), np.float32)
    sblk = (np.arange(128) // 64)[:, None]
    tblk = (np.arange(128) // 64)[None, :]
    m[:] = np.where(sblk <= tblk, 1.0, 0.0)[:, None, :]
    return c, m.reshape(128, 1024)


_NC_CACHE = {}


def kernel(x, norm_g, w_in, sgu_ln_g, sgu_ln_b, w_spatial, b_spatial, w_up_a, w_up_b, w_out, final_norm_g):
    x = np.asarray(x, np.float32)
    B, S, _ = x.shape
    f32 = lambda a: np.ascontiguousarray(np.asarray(a, np.float32))
    cst, msk = _consts()
    g1 = f32(np.asarray(norm_g)[0].reshape(8, 128).T)
    fg = f32(np.broadcast_to(np.asarray(final_norm_g).reshape(1, D), (128, D)))
    lg = f32(np.concatenate([np.asarray(sgu_ln_g)[0].reshape(4, 128).T,
                             np.asarray(sgu_ln_b)[0].reshape(4, 128).T], axis=1))
    bs = np.asarray(b_spatial)[0]
    bsT = np.empty((128, 4, 128), np.float32)
    for g in range(8):
        bsT[(g % 2) * 64:(g % 2) * 64 + 64, g // 2, :] = bs[g][None, :]
    wsT = f32(np.transpose(np.asarray(w_spatial)[0], (2, 0, 1)).reshape(128, 1024))
    common = {
        "w_in": f32(np.asarray(w_in)[0]), "w_up_a": f32(np.asarray(w_up_a)[0]),
        "w_up_b": f32(np.asarray(w_up_b)[0]), "w_out": f32(np.asarray(w_out)[0]),
        "g1": g1, "fg": fg, "lg": lg, "bsT": f32(bsT.reshape(128, 512)),
        "wsT": wsT, "msk": f32(msk), "cst": f32(cst),
    }
    if S not in _NC_CACHE:
        _NC_CACHE[S] = build_nc(S)
    nc = _NC_CACHE[S]
    in_maps = []
    for b in range(B):
        m = dict(common)
        m["x"] = f32(x[b])
        in_maps.append(m)
    res = run_bass_kernel_spmd(nc, in_maps, core_ids=list(range(B)))
    return np.stack([np.asarray(r["out"], np.float32).reshape(S, D) for r in res.results], axis=0)
```
